# Optimizing a Trainium2 kernel written in Bass

```python
import math
import jax, jax.numpy as jnp
from jax import lax
import numpy as np

D_MODEL = 1024
BATCH = 8
SEQ = 4096
DEPTH = 2

CHUNK = 64
Q_BLOCK = 128
N_MIXERS = 2
ATTN_HEADS = 8
ATTN_HEAD_DIM = D_MODEL // (2 * ATTN_HEADS)
ATTN_V_DIM = 2 * ATTN_HEAD_DIM
ROPE_THETA = 10000.0
LAMBDA_STD = 0.1
D_RNN = D_MODEL
RG_BLOCK = 256
RG_HEADS = D_RNN // RG_BLOCK
CONV_WIDTH = 4
RG_C = 8.0
D_FF = 4 * D_MODEL
NORM_EPS = 1e-6
SUBLN_EPS = 1e-5

kernel_name = "hybrid_diffattn_rglru_streaming"


def rmsnorm(x, g, eps=NORM_EPS):
    xf = x.astype(jnp.float32)
    y = xf * lax.rsqrt(jnp.mean(xf * xf, axis=-1, keepdims=True) + eps)
    return (y * g.astype(jnp.float32)).astype(x.dtype)


def rope(t, positions):
    d = t.shape[-1]
    inv_freq = 1.0 / (ROPE_THETA ** (jnp.arange(0, d, 2, dtype=jnp.float32) / d))
    ang = positions.astype(jnp.float32)[:, None] * inv_freq[None, :]
    cos = jnp.cos(ang)[:, None, None, :]
    sin = jnp.sin(ang)[:, None, None, :]
    tf = t.astype(jnp.float32)
    t1, t2 = tf[..., : d // 2], tf[..., d // 2:]
    out = jnp.concatenate([t1 * cos - t2 * sin, t2 * cos + t1 * sin], axis=-1)
    return out.astype(t.dtype)


def diff_attention(h, w_qkv, w_o, lq1, lk1, lq2, lk2, subln_g, lambda_init):
    B, S, _ = h.shape
    H, d = ATTN_HEADS, ATTN_HEAD_DIM
    qkv = h @ w_qkv
    q, k, v = jnp.split(qkv, 3, axis=-1)
    positions = jnp.arange(S, dtype=jnp.int32)
    q = rope(q.reshape(B, S, H, 2, d), positions) * (d ** -0.5)
    k = rope(k.reshape(B, S, H, 2, d), positions)
    v = v.reshape(B, S, H, ATTN_V_DIM)
    lam = (jnp.exp(jnp.sum(lq1.astype(jnp.float32) * lk1.astype(jnp.float32)))
           - jnp.exp(jnp.sum(lq2.astype(jnp.float32) * lk2.astype(jnp.float32)))
           + lambda_init)
    n_blk = S // Q_BLOCK
    q_blocks = q.reshape(B, n_blk, Q_BLOCK, H, 2, d).transpose(1, 0, 2, 3, 4, 5)
    k_chunk = jnp.arange(S) // CHUNK

    def one_block(args):
        qb, bi = args
        q_chunk = (bi * Q_BLOCK + jnp.arange(Q_BLOCK)) // CHUNK
        allowed = k_chunk[None, :] <= q_chunk[:, None]
        s = jnp.einsum('bqhcd,bkhcd->bhcqk', qb, k).astype(jnp.float32)
        s = jnp.where(allowed[None, None, None], s, -jnp.inf)
        p = jax.nn.softmax(s, axis=-1)
        a = p[:, :, 0] - lam * p[:, :, 1]
        return jnp.einsum('bhqk,bkhe->bqhe', a.astype(v.dtype), v)

    o = lax.map(one_block, (q_blocks, jnp.arange(n_blk)))
    o = o.transpose(1, 0, 2, 3, 4).reshape(B, S, H, ATTN_V_DIM)
    o = rmsnorm(o, subln_g, SUBLN_EPS) * (1.0 - lambda_init)
    return o.reshape(B, S, H * ATTN_V_DIM) @ w_o


def causal_depthwise_conv(x, w, b):
    S = x.shape[1]
    xp = jnp.pad(x, ((0, 0), (CONV_WIDTH - 1, 0), (0, 0)))
    y = sum(xp[:, j:j + S] * w[j] for j in range(CONV_WIDTH))
    return y + b


def _lru_combine(left, right):
    a1, b1 = left
    a2, b2 = right
    return a1 * a2, a2 * b1 + b2


def recurrent_block(h, w_x, w_y, conv_w, conv_b, w_a, b_a, w_i, b_i, lam_param, w_o):
    B, S, _ = h.shape
    gate_branch = jax.nn.gelu(h @ w_y)
    xb = causal_depthwise_conv(h @ w_x, conv_w, conv_b)
    xg = xb.reshape(B, S, RG_HEADS, RG_BLOCK)
    r = jax.nn.sigmoid((jnp.einsum('bsnc,ncd->bsnd', xg, w_a).reshape(B, S, D_RNN) + b_a).astype(jnp.float32))
    i = jax.nn.sigmoid((jnp.einsum('bsnc,ncd->bsnd', xg, w_i).reshape(B, S, D_RNN) + b_i).astype(jnp.float32))
    log_a = -RG_C * r * jax.nn.softplus(-lam_param.astype(jnp.float32))
    a = jnp.exp(log_a)
    mult = jnp.sqrt(-jnp.expm1(2.0 * log_a))
    u = mult * (i * xb.astype(jnp.float32))
    _, hs = lax.associative_scan(_lru_combine, (a, u), axis=1)
    return (hs.astype(h.dtype) * gate_branch) @ w_o


def sqrelu_mlp(h, w1, w2):
    return jnp.square(jax.nn.relu(h @ w1)) @ w2


def setup_inputs(seed: int = 0) -> dict:
    key = jax.random.key(seed)
    ks = iter(jax.random.split(key, 32))
    n_attn = (DEPTH + 1) // 2
    n_rec = DEPTH // 2
    f32 = jnp.float32

    def nrm(shape, fan_in):
        return jax.random.normal(next(ks), shape, f32) * (fan_in ** -0.5)

    def gain(shape):
        return 1.0 + 0.02 * jax.random.normal(next(ks), shape, f32)

    x = jax.random.normal(next(ks), (BATCH, SEQ, D_MODEL), f32)
    mix_norm_g = gain((DEPTH, D_MODEL))
    mlp_norm_g = gain((DEPTH, D_MODEL))
    attn_w_qkv = nrm((n_attn, D_MODEL, 3 * D_MODEL), D_MODEL)
    attn_w_o = nrm((n_attn, ATTN_HEADS * ATTN_V_DIM, D_MODEL), ATTN_HEADS * ATTN_V_DIM)
    attn_lq1 = LAMBDA_STD * jax.random.normal(next(ks), (n_attn, ATTN_HEAD_DIM), f32)
    attn_lk1 = LAMBDA_STD * jax.random.normal(next(ks), (n_attn, ATTN_HEAD_DIM), f32)
    attn_lq2 = LAMBDA_STD * jax.random.normal(next(ks), (n_attn, ATTN_HEAD_DIM), f32)
    attn_lk2 = LAMBDA_STD * jax.random.normal(next(ks), (n_attn, ATTN_HEAD_DIM), f32)
    attn_subln_g = gain((n_attn, ATTN_V_DIM))
    rec_w_x = nrm((n_rec, D_MODEL, D_RNN), D_MODEL)
    rec_w_y = nrm((n_rec, D_MODEL, D_RNN), D_MODEL)
    rec_conv_w = nrm((n_rec, CONV_WIDTH, D_RNN), CONV_WIDTH)
    rec_conv_b = 0.01 * jax.random.normal(next(ks), (n_rec, D_RNN), f32)
    rec_w_a = nrm((n_rec, RG_HEADS, RG_BLOCK, RG_BLOCK), RG_BLOCK)
    rec_b_a = 0.01 * jax.random.normal(next(ks), (n_rec, D_RNN), f32)
    rec_w_i = nrm((n_rec, RG_HEADS, RG_BLOCK, RG_BLOCK), RG_BLOCK)
    rec_b_i = 0.01 * jax.random.normal(next(ks), (n_rec, D_RNN), f32)
    u = jax.random.uniform(next(ks), (n_rec, D_RNN), f32, 0.9, 0.999)
    a_base = u ** (1.0 / RG_C)
    rec_lambda = jnp.log(a_base) - jnp.log1p(-a_base)
    rec_w_o = nrm((n_rec, D_RNN, D_MODEL), D_RNN)
    mlp_w1 = nrm((DEPTH, D_MODEL, D_FF), D_MODEL)
    mlp_w2 = nrm((DEPTH, D_FF, D_MODEL), D_FF)
    final_norm_g = gain((D_MODEL,))
    return {"x": x, "mix_norm_g": mix_norm_g, "mlp_norm_g": mlp_norm_g,
            "attn_w_qkv": attn_w_qkv, "attn_w_o": attn_w_o,
            "attn_lq1": attn_lq1, "attn_lk1": attn_lk1, "attn_lq2": attn_lq2, "attn_lk2": attn_lk2,
            "attn_subln_g": attn_subln_g,
            "rec_w_x": rec_w_x, "rec_w_y": rec_w_y, "rec_conv_w": rec_conv_w, "rec_conv_b": rec_conv_b,
            "rec_w_a": rec_w_a, "rec_b_a": rec_b_a, "rec_w_i": rec_w_i, "rec_b_i": rec_b_i,
            "rec_lambda": rec_lambda, "rec_w_o": rec_w_o,
            "mlp_w1": mlp_w1, "mlp_w2": mlp_w2, "final_norm_g": final_norm_g}


def reference(x, mix_norm_g, mlp_norm_g, attn_w_qkv, attn_w_o, attn_lq1, attn_lk1, attn_lq2, attn_lk2,
              attn_subln_g, rec_w_x, rec_w_y, rec_conv_w, rec_conv_b, rec_w_a, rec_b_a, rec_w_i, rec_b_i,
              rec_lambda, rec_w_o, mlp_w1, mlp_w2, final_norm_g):
    for layer in range(DEPTH):
        h = rmsnorm(x, mix_norm_g[layer])
        j = layer // N_MIXERS
        if layer % N_MIXERS == 0:
            lambda_init = 0.8 - 0.6 * math.exp(-0.3 * layer)
            x = x + diff_attention(h, attn_w_qkv[j], attn_w_o[j], attn_lq1[j], attn_lk1[j],
                                   attn_lq2[j], attn_lk2[j], attn_subln_g[j], lambda_init)
        else:
            x = x + recurrent_block(h, rec_w_x[j], rec_w_y[j], rec_conv_w[j], rec_conv_b[j],
                                    rec_w_a[j], rec_b_a[j], rec_w_i[j], rec_b_i[j],
                                    rec_lambda[j], rec_w_o[j])
        x = x + sqrelu_mlp(rmsnorm(x, mlp_norm_g[layer]), mlp_w1[layer], mlp_w2[layer])
    return rmsnorm(x, final_norm_g)
```

```python
import numpy as np
from contextlib import ExitStack
import concourse.bass as bass
import concourse.mybir as mybir
from concourse.bass_utils import run_bass_kernel_spmd

F32 = mybir.dt.float32
BF16 = mybir.dt.bfloat16
AF = mybir.ActivationFunctionType
ALU = mybir.AluOpType
AX = mybir.AxisListType


class Tok:
    __slots__ = ("name", "last_w", "readers")

    def __init__(self, name):
        self.name = name
        self.last_w = None
        self.readers = []


class Chan:
    def __init__(self, sem, name):
        self.sem = sem
        self.name = name
        self.count = 0


class Op:
    __slots__ = ("eng", "fn", "reads", "writes", "chan", "deps", "sig", "sigval", "idx")


class Sched:
    ENGS = ("sp", "act", "dve", "pool", "pe")

    def __init__(self, nc, stack):
        self.nc = nc
        self.stack = stack
        self.e = {"sp": nc.sync, "act": nc.scalar, "dve": nc.vector, "pool": nc.gpsimd, "pe": nc.tensor}
        self.prog = {k: stack.enter_context(nc.semaphore("prog_" + k)) for k in self.ENGS if k != "sp"}
        self.sigcount = {k: 0 for k in self.ENGS}
        self.waited = {k: {} for k in self.ENGS}
        self.ops = []
        self.chans = []
        self.nsem = 4

    def chan(self, name):
        c = Chan(self.stack.enter_context(self.nc.semaphore("ch_" + name)), name)
        self.chans.append(c)
        self.nsem += 1
        return c

    def op(self, eng, fn, reads=(), writes=(), chan=None):
        o = Op()
        o.eng, o.fn, o.reads, o.writes, o.chan = eng, fn, tuple(reads), tuple(writes), chan
        o.deps, o.sig, o.sigval = [], False, None
        self.ops.append(o)
        return o

    def dma(self, eng, out, in_, chan, reads=(), writes=()):
        return self.op(eng, lambda e: e.dma_start(out=out, in_=in_), reads, writes, chan)

    def flush(self, barrier=True):
        ops = self.ops
        for k, o in enumerate(ops):
            o.idx = k
            deps = set()
            for r in o.reads:
                if r.last_w is not None:
                    deps.add(r.last_w)
            for w in o.writes:
                if w.last_w is not None:
                    deps.add(w.last_w)
                for rd in w.readers:
                    deps.add(rd)
            deps.discard(k)
            for d in sorted(deps):
                od = ops[d]
                if od.eng == o.eng == "pe" and od.chan is None and o.chan is None:
                    continue
                o.deps.append(od)
                if od.chan is None:
                    od.sig = True
            for r in o.reads:
                r.readers.append(k)
            for w in o.writes:
                w.last_w = k
                w.readers = []
        for o in ops:
            eng = self.e[o.eng]
            wt = self.waited[o.eng]
            for od in o.deps:
                if od.chan is not None:
                    sem, val = od.chan.sem, od.chan.count
                else:
                    sem, val = self.prog[od.eng], od.sigval
                assert val is not None
                key = id(sem)
                if wt.get(key, 0) >= val:
                    continue
                wt[key] = val
                eng.wait_ge(sem, val)
            inst = o.fn(eng)
            if o.chan is not None:
                o.chan.count += 16
                o.sigval = o.chan.count
                inst.then_inc(o.chan.sem, 16)
                o.chan.last_eng = o.eng
            elif o.sig:
                self.sigcount[o.eng] += 1
                o.sigval = self.sigcount[o.eng]
                inst.then_inc(self.prog[o.eng], 1)
        for c in self.chans:
            if c.count and getattr(c, "last_eng", None) is not None:
                wt = self.waited[c.last_eng]
                if wt.get(id(c.sem), 0) < c.count:
                    wt[id(c.sem)] = c.count
                    self.e[c.last_eng].wait_ge(c.sem, c.count)
        if barrier:
            self.nc.all_engine_barrier()
        seen = set()
        for o in ops:
            for t in o.reads + o.writes:
                if id(t) not in seen:
                    seen.add(id(t))
                    t.last_w = None
                    t.readers = []
        self.ops = []


class TK:
    def __init__(self):
        self.d = {}

    def __call__(self, *key):
        t = self.d.get(key)
        if t is None:
            t = self.d[key] = Tok(str(key))
        return t


D = 1024
DFF = 4096
NH = 8
EPS = 1e-6
SUB_EPS = 1e-5
LAMBDA_INIT = 0.8 - 0.6 * 1.0
GK = 0.7978845608028654 * 2.0
GC = 0.044715


def build_nc(NT=4096, debug=False, phases="ABCDE"):
    NTL = NT // 128
    nc = bass.Bass("TRN2", target_bir_lowering=False)
    dt_in = lambda n, s: nc.dram_tensor(n, list(s), F32, kind="ExternalInput").ap()
    skind = "ExternalOutput" if debug else "Internal"
    dt_s = lambda n, s, d: nc.dram_tensor(n, list(s), d, kind=skind).ap()
    x_d = dt_in("x", [NT, D])
    wqkv_d = dt_in("wqkv", [D, 3 * D]); woa_d = dt_in("woa", [D, D])
    w1_d = [dt_in("w1_0", [D, DFF]), dt_in("w1_1", [D, DFF])]
    w2_d = [dt_in("w2_0", [DFF, D]), dt_in("w2_1", [DFF, D])]
    wx_d = dt_in("wx", [D, D]); wy_d = dt_in("wy", [D, D]); wor_d = dt_in("wor", [D, D])
    wa_d = dt_in("wa", [4, 256, 256]); wi_d = dt_in("wi", [4, 256, 256])
    gvec_d = dt_in("gvec", [5, D])
    cvec_d = dt_in("cvec", [128, 96])
    lamv_d = dt_in("lamv", [1, 256]); subg_d = dt_in("subg", [1, 128])
    rope_d = dt_in("rope", [NT, 128]); ident_d = dt_in("ident", [128, 128])
    out_d = nc.dram_tensor("out", [NT, D], F32, kind="ExternalOutput").ap()
    QT_s = dt_s("QT_s", [NH, 128, NT], BF16); KT_s = dt_s("KT_s", [NH, 128, NT], BF16)
    V_s = dt_s("V_s", [NH, 128, NTL, 129], BF16)
    OT_s = dt_s("OT_s", [NH, 128, NT], BF16)
    x2_s = dt_s("x2_s", [NT, D], F32)
    ZT_s = dt_s("ZT_s", [8, 128, NT], BF16)

    with ExitStack() as top:
        S = Sched(nc, top)
        tk = TK()
        ucnt = [0]

        def sbt(st, n, s, d):
            ucnt[0] += 1
            return st.enter_context(nc.sbuf_tensor(f"s{ucnt[0]}_{n}", list(s), d))

        def pst(st, n, s, d):
            ucnt[0] += 1
            return st.enter_context(nc.psum_tensor(f"p{ucnt[0]}_{n}", list(s), d))
        ident = sbt(top, "ident", [128, 128], BF16)
        eps6 = sbt(top, "eps6", [128, 1], F32); eps5 = sbt(top, "eps5", [128, 1], F32)
        cvec = sbt(top, "cvec", [128, 96], F32)
        c_const = S.chan("const")
        c_const2 = S.chan("const2")
        S.dma("pool", ident[:], ident_d, c_const2, writes=[tk("ident")])
        S.dma("sp", cvec[:], cvec_d, c_const, writes=[tk("cvec")])
        S.op("dve", lambda e: e.memset(eps6[:], EPS), writes=[tk("eps6")])
        S.op("dve", lambda e: e.memset(eps5[:], SUB_EPS), writes=[tk("eps5")])
        S.flush()
        CONST_R = [tk("ident"), tk("eps6"), tk("eps5"), tk("cvec")]

        def rstd_ops(ss_ap, out_ap, n, eps_t, toks_r, toks_w):
            S.op("act", lambda e: e.activation(out=out_ap, in_=ss_ap, func=AF.Ln, scale=1.0 / n, bias=eps_t[:]), reads=toks_r, writes=toks_w)
            S.op("act", lambda e: e.activation(out=out_ap, in_=out_ap, func=AF.Exp, scale=-0.5), reads=toks_w, writes=toks_w)

        def gcol_bc(g_row, w=128):
            return cvec[:, 64 + 8 * g_row:72 + 8 * g_row].unsqueeze(2).to_broadcast([128, 8, w])

        def prologue_act(st_bufs, key, xt_ap):
            B = st_bufs
            s = B["pi"] % 2
            B["pi"] += 1
            ss, rstd, xb = B["ss"][s], B["rstd"][s], B["xb"][s]
            S.op("act", lambda e: e.activation(out=xb[:], in_=xt_ap, func=AF.Square, accum_out=ss[:]),
                 reads=[key], writes=[tk("xb", s), tk("ss", s)])
            rstd_ops(ss[:], rstd[:], float(D), eps6, [tk("ss", s)], [tk("rstd", s)])
            S.op("act", lambda e: e.activation(out=xb[:], in_=xt_ap, func=AF.Copy, scale=rstd[:]),
                 reads=[key, tk("rstd", s)], writes=[tk("xb", s)])
            return s

        def prologue_pe(st_bufs, s):
            B = st_bufs
            xb, tp = B["xb"][s], B["tp"][s % len(B["tp"])]
            tps = s % len(B["tp"])

            def tr(e):
                for c in range(8):
                    i = e.transpose(out=tp[:, c, :], in_=xb[:, c * 128:(c + 1) * 128], identity=ident[:])
                return i
            S.op("pe", tr, reads=[tk("xb", s)], writes=[tk("tp", tps)])
            return tp, tk("tp", tps)

        def prologue(st_bufs, key, xt_ap, xT_dst, gbc):
            s = prologue_act(st_bufs, key, xt_ap)
            return prologue_pe(st_bufs, s)

        def mk_prologue_bufs(st, ntp=1):
            B = {"pi": 0}
            B["ss"] = [sbt(st, f"ss{i}", [128, 1], F32) for i in range(2)]
            B["rstd"] = [sbt(st, f"rstd{i}", [128, 1], F32) for i in range(2)]
            B["xb"] = [sbt(st, f"xb{i}", [128, D], BF16) for i in range(2)]
            B["tp"] = [pst(st, f"tp{i}", [128, 8, 128], BF16) for i in range(ntp)]
            return B

        if "A" in phases:
            with ExitStack() as st:
                wq = sbt(st, "wqkv_b", [128, 8, 3 * D], BF16)
                gbc = None
                ropeT = sbt(st, "ropeT", [128, NTL, 128], F32)
                PB = mk_prologue_bufs(st, 1)
                xt = [sbt(st, f"xt{i}", [128, D], F32) for i in range(2)]
                xT = [sbt(st, f"xT{i}", [128, 8, 128], BF16) for i in range(2)]
                ksb = sbt(st, "ksb", [128, D], F32)
                rA = [sbt(st, f"rA{i}", [128, D], F32) for i in range(2)]
                rB = [sbt(st, f"rB{i}", [128, D], F32) for i in range(2)]
                qkb = [[sbt(st, f"qkb{w}{i}", [128, D], BF16) for i in range(2)] for w in range(2)]
                qT_acc = [sbt(st, f"qTa{i}", [128, 8, 512], BF16) for i in range(2)]
                kT_acc = [sbt(st, f"kTa{i}", [128, 8, 512], BF16) for i in range(2)]
                V_acc = [sbt(st, f"Va{i}", [128, 8, 4, 129], BF16) for i in range(2)]
                pq = pst(st, "pq", [128, D], F32); pk = pst(st, "pk", [128, D], F32); pv = pst(st, "pv", [128, D], F32)
                tq = pst(st, "tq", [128, 8, 128], BF16)
                cwq = [S.chan(f"A_w{i}") for i in range(3)]; cg = S.chan("A_g"); cx = [S.chan("A_x0"), S.chan("A_x1")]
                cst = [S.chan("A_st0"), S.chan("A_st1")]
                for cb6 in range(6):
                    S.dma("pool", wq[:, :, cb6 * 512:(cb6 + 1) * 512], wqkv_d[:, cb6 * 512:(cb6 + 1) * 512].rearrange("(c p) f -> p c f", p=128), cwq[cb6 // 2], writes=[tk("wq", cb6)])
                S.dma("sp", ropeT[:], rope_d.rearrange("(i p) f -> p i f", p=128), cg, writes=[tk("ropeT")])
                for i in range(2):
                    S.op("dve", lambda e, i=i: e.memset(V_acc[i][:, :, :, 128:129], 1.0), writes=[tk("Vacc", i)])

                def rope(eng, src_ap, src_tok, dst, dst_tok, i, slot):
                    sv = src_ap.rearrange("p (g d) -> p g d", d=64)
                    cosb = ropeT[:, i, 0:64].unsqueeze(1).to_broadcast([128, 16, 64])
                    sinlo = ropeT[:, i, 64:96].unsqueeze(1).to_broadcast([128, 16, 32])
                    sinhi = ropeT[:, i, 96:128].unsqueeze(1).to_broadcast([128, 16, 32])
                    A = rA[slot][:].rearrange("p (g d) -> p g d", d=64)
                    Bv = rB[slot][:].rearrange("p (g d) -> p g d", d=64)
                    S.op(eng, lambda e: e.tensor_tensor(out=A, in0=sv, in1=cosb, op=ALU.mult),
                         reads=[src_tok, tk("ropeT")], writes=[tk("rA", slot)])
                    S.op(eng, lambda e: e.tensor_tensor(out=Bv[:, :, 0:32], in0=sv[:, :, 32:64], in1=sinlo, op=ALU.mult),
                         reads=[src_tok, tk("ropeT")], writes=[tk("rB", slot, 0)])
                    S.op(eng, lambda e: e.tensor_tensor(out=Bv[:, :, 32:64], in0=sv[:, :, 0:32], in1=sinhi, op=ALU.mult),
                         reads=[src_tok, tk("ropeT")], writes=[tk("rB", slot, 1)])
                    S.op(eng, lambda e: e.tensor_tensor(out=dst[:], in0=rA[slot][:], in1=rB[slot][:], op=ALU.add),
                         reads=[tk("rA", slot), tk("rB", slot, 0), tk("rB", slot, 1)], writes=[dst_tok])

                def proA1(i):
                    s = i % 2
                    S.dma("sp", xt[s][:], x_d[i * 128:(i + 1) * 128, :], cx[s], writes=[tk("xt", s)])
                    return prologue_act(PB, tk("xt", s), xt[s][:])

                def proA2(i, ps):
                    s = i % 2
                    tp, tptok = prologue_pe(PB, ps)
                    S.op("dve", lambda e, s=s, tp=tp: e.tensor_tensor(out=xT[s][:], in0=tp[:], in1=gcol_bc(0), op=ALU.mult), reads=[tptok], writes=[tk("xT", s)])

                def mmA(i):
                    s = i % 2
                    for (pp, name, off) in ((pq, "pq", 0), (pk, "pk", D), (pv, "pv", 2 * D)):
                        def mm(e, pp=pp, off=off, s=s):
                            for n in range(2):
                                for c in range(8):
                                    r = e.matmul(pp[:, n * 512:(n + 1) * 512], lhsT=xT[s][:, c, :], rhs=wq[:, c, off + n * 512: off + (n + 1) * 512],
                                                 start=(c == 0), stop=(c == 7))
                            return r
                        S.op("pe", mm, reads=[tk("xT", s), tk("wq", off // 512), tk("wq", off // 512 + 1)], writes=[tk(name)])

                def evacA1(i):
                    s = i % 2; gi = i // 4; gs = gi % 2; sub = i % 4
                    rope("dve", pq[:], tk("pq"), qkb[0][i % 2], tk("qkb", 0, i % 2), i, 0)
                    S.op("act", lambda e: e.activation(out=ksb[:], in_=pk[:], func=AF.Copy), reads=[tk("pk")], writes=[tk("ksb")])
                    rope("pool", ksb[:], tk("ksb"), qkb[1][i % 2], tk("qkb", 1, i % 2), i, 1)
                    S.op("act", lambda e, gs=gs, sub=sub: e.activation(out=V_acc[gs][:, :, sub, 0:128], in_=pv[:].rearrange("p (h e) -> p h e", e=128), func=AF.Copy),
                         reads=[tk("pv")], writes=[tk("Vacc", gs)])

                def evacA2(i):
                    s = i % 2; gi = i // 4; gs = gi % 2; sub = i % 4
                    for (which, acc, accn) in ((0, qT_acc, "qTa"), (1, kT_acc, "kTa")):
                        def trq(e, which=which, i=i):
                            for h in range(8):
                                r = e.transpose(out=tq[:, h, :], in_=qkb[which][i % 2][:, h * 128:(h + 1) * 128], identity=ident[:])
                            return r
                        S.op("pe", trq, reads=[tk("qkb", which, i % 2)], writes=[tk("tq")])
                        ceng = "dve" if which == 0 else "act"
                        if ceng == "dve":
                            S.op("dve", lambda e, acc=acc, gs=gs, sub=sub: e.tensor_copy(out=acc[gs][:, :, sub * 128:(sub + 1) * 128], in_=tq[:]),
                                 reads=[tk("tq")], writes=[tk(accn, gs)])
                        else:
                            S.op("act", lambda e, acc=acc, gs=gs, sub=sub: e.activation(out=acc[gs][:, :, sub * 128:(sub + 1) * 128], in_=tq[:], func=AF.Copy),
                                 reads=[tk("tq")], writes=[tk(accn, gs)])
                    if sub == 3 or i == NTL - 1:
                        nt = (sub + 1) * 128
                        t0 = gi * 512
                        S.dma("pool", QT_s[:, :, t0:t0 + nt].rearrange("h p t -> p h t"), qT_acc[gs][:, :, 0:nt], cst[gs],
                              reads=[tk("qTa", gs)], writes=[tk("QT_s", gi)])
                        S.dma("pool", KT_s[:, :, t0:t0 + nt].rearrange("h p t -> p h t"), kT_acc[gs][:, :, 0:nt], cst[gs],
                              reads=[tk("kTa", gs)], writes=[tk("KT_s", gi)])
                        S.dma("pool", V_s[:, :, gi * 4:gi * 4 + sub + 1, :].rearrange("h p j e -> p h j e"), V_acc[gs][:, :, 0:sub + 1, :], cst[gs],
                              reads=[tk("Vacc", gs)], writes=[tk("V_s", gi)])


                proA2(0, proA1(0)); mmA(0)
                psn = proA1(1) if NTL > 1 else None
                for i in range(NTL):
                    evacA1(i)
                    if i + 1 < NTL:
                        proA2(i + 1, psn)
                        mmA(i + 1)
                    if i + 2 < NTL:
                        psn = proA1(i + 2)
                    evacA2(i)
                S.flush()

        sc1 = ExitStack(); sc2 = ExitStack()
        W1b = sbt(sc1, "W1b", [128, 8, DFF], BF16); Wzb = sbt(sc1, "Wzb", [128, 8, D], BF16)
        W2box = [sbt(sc2, "W2b", [128, 32, D], BF16)]
        cwm = [S.chan("M_w1"), S.chan("M_w2"), S.chan("M_wz")]

        def load_w1z(w1d, wz_d):
            for c in range(8):
                S.dma("pool", Wzb[:, c, :], wz_d[c * 128:(c + 1) * 128, :], cwm[2], writes=[tk("Wz", c)])
            for c in range(8):
                S.dma("pool", W1b[:, c, :], w1d[c * 128:(c + 1) * 128, :], cwm[0], writes=[tk("W1", c)])

        def load_w2(w2d):
            for c4 in range(8):
                S.dma("pool", W2box[0][:, c4 * 4:(c4 + 1) * 4, :], w2d[c4 * 512:(c4 + 1) * 512, :].rearrange("(c p) d -> p c d", p=128), cwm[1], writes=[tk("W2", c4)])

        if "B" in phases:
            with ExitStack() as st:
                QB = 512
                NQ = NT // QB
                NG = (NT + 511) // 512
                K0p = sbt(st, "K0p", [128, NT], BF16)
                K1p = sbt(st, "K1p", [128, NT], BF16)
                QTq = [sbt(st, f"QTq{i}", [128, QB], BF16) for i in range(2)]
                V_h = sbt(st, "Vh", [128, NTL, 129], BF16)
                NPT = 3
                PT = [sbt(st, f"PT{i}", [128, 2, QB], BF16) for i in range(NPT)]
                lamv = sbt(st, "lamv", [128, 256], F32); g8col = sbt(st, "g8col", [128, 1], F32)
                ltmp = sbt(st, "ltmp", [128, 128], F32); lsum = sbt(st, "lsum", [128, 2], F32)
                nlam = sbt(st, "nlam", [128, 1], F32)
                ones_b = sbt(st, "ones_b", [128, 128], BF16)
                ones_f = sbt(st, "ones_f", [128, 128], F32)
                Oc = [sbt(st, f"Oc{i}", [128, 2, QB], F32) for i in range(2)]
                bcs = [sbt(st, f"bcs{i}", [128, 2, QB], F32) for i in range(2)]
                sq = [sbt(st, f"sq{i}", [128, QB], BF16) for i in range(2)]
                rr = [sbt(st, f"rr{i}", [128, QB], F32) for i in range(2)]
                OTo = [sbt(st, f"OTo{i}", [128, QB], BF16) for i in range(2)]
                Sps = pst(st, "Sps", [128, 2, 2, 512], F32)
                OTp = pst(st, "OTp", [128, 2, 512], F32)
                denp = pst(st, "denp", [128, 2, 512], F32)
                ck = [S.chan(f"B_k{g}") for g in range(NG)]; cv = [S.chan(f"B_v{g}") for g in range(NG)]; cq = [S.chan("B_q0"), S.chan("B_q1")]
                cl = S.chan("B_l"); cs2 = S.chan("B_s"); co = [S.chan("B_o0"), S.chan("B_o1")]; cb = [S.chan("B_b0"), S.chan("B_b1")]
                if "C" in phases:
                    load_w1z(w1_d[0], woa_d); load_w2(w2_d[0])
                S.dma("sp", lamv[:], lamv_d.partition_broadcast(128), cl, writes=[tk("lamv")])
                with nc.allow_non_contiguous_dma(reason="tiny 128-element column load"):
                    S.dma("sp", g8col[:], subg_d.rearrange("o e -> e o"), cs2, writes=[tk("g8col")])
                for i in range(2):
                    S.op("dve", lambda e, i=i: e.tensor_tensor(out=ltmp[:, i * 64:(i + 1) * 64], in0=lamv[:, i * 128:i * 128 + 64],
                                                                in1=lamv[:, i * 128 + 64:i * 128 + 128], op=ALU.mult),
                         reads=[tk("lamv")], writes=[tk("ltmp")])
                    S.op("dve", lambda e, i=i: e.reduce_sum(out=lsum[:, i:i + 1], in_=ltmp[:, i * 64:(i + 1) * 64], axis=AX.X),
                         reads=[tk("ltmp")], writes=[tk("lsum")])
                S.op("act", lambda e: e.activation(out=lsum[:], in_=lsum[:], func=AF.Exp), reads=[tk("lsum")], writes=[tk("lsum")])
                S.op("dve", lambda e: e.tensor_tensor(out=nlam[:], in0=lsum[:, 1:2], in1=lsum[:, 0:1], op=ALU.subtract),
                     reads=[tk("lsum")], writes=[tk("nlam")])
                S.op("dve", lambda e: e.tensor_scalar(out=nlam[:], in0=nlam[:], scalar1=-LAMBDA_INIT, scalar2=None, op0=ALU.add),
                     reads=[tk("nlam")], writes=[tk("nlam")])
                S.op("dve", lambda e: e.tensor_scalar(out=g8col[:], in0=g8col[:], scalar1=1.0 - LAMBDA_INIT, scalar2=None, op0=ALU.mult),
                     reads=[tk("g8col")], writes=[tk("g8col")])
                S.op("dve", lambda e: e.memset(ones_b[:], 1.0), writes=[tk("ones_b")])
                S.op("dve", lambda e: e.memset(ones_f[:], 1.0), writes=[tk("ones_f")])
                S.op("pool", lambda e: e.memset(K0p[64:128, :], 0.0), writes=[tk("Kz")])
                S.op("pool", lambda e: e.memset(K1p[0:64, :], 0.0), writes=[tk("Kz")])

                def load_kv(hh, g):
                    t0 = g * 512; t1 = min(NT, t0 + 512)
                    S.dma("sp", K0p[0:64, t0:t1], KT_s[hh, 0:64, t0:t1], ck[g], writes=[tk("K", g)])
                    S.dma("sp", K1p[64:128, t0:t1], KT_s[hh, 64:128, t0:t1], ck[g], writes=[tk("K", g)])
                    S.dma("sp", V_h[:, t0 // 128:t1 // 128, :], V_s[hh, :, t0 // 128:t1 // 128, :], cv[g], writes=[tk("V", g)])
                SI = [0]
                pidx = 0
                bidx = 0
                pending = []
                GJ = [0]
                t2b = {}

                def release(boundary):
                    held = set()
                    for ent in list(pending):
                        bid_ = ent["bid"]
                        if bid_ in held:
                            continue
                        ok = ent["rel"] <= GJ[0]
                        if ent["k"] == 3 and boundary and bid_ in t2b and GJ[0] >= t2b[bid_] + 2:
                            ok = True
                        if not ok:
                            held.add(bid_)
                            continue
                        pending.remove(ent)
                        ent["fn"]()
                        if ent["k"] == 2:
                            t2b[bid_] = GJ[0]
                        if ent["k"] == 3:
                            for e2 in pending:
                                if e2["bid"] == bid_ and e2["k"] == 4:
                                    e2["rel"] = GJ[0] + 2

                def make_epilogue(h, Q, eb):
                    t0 = Q * QB
                    def s0():
                        for c in range(2):
                            S.op("dve", lambda e, c=c: e.tensor_copy(out=Oc[eb][:, c, :], in_=OTp[:, c, :]), reads=[tk("OTp", c)], writes=[tk("Oc", eb)])
                        for c in range(2):
                            S.op("dve", lambda e, c=c: e.tensor_copy(out=bcs[eb][:, c, :], in_=denp[:, c, :]), reads=[tk("denp", c)], writes=[tk("bcs", eb)])
                    def s1():
                        S.op("dve", lambda e: e.reciprocal(out=bcs[eb][:], in_=bcs[eb][:]), reads=[tk("bcs", eb)], writes=[tk("bcs", eb)])
                        S.op("dve", lambda e: e.tensor_scalar(out=bcs[eb][:, 1, :], in0=bcs[eb][:, 1, :], scalar1=nlam[:], scalar2=None, op0=ALU.mult),
                             reads=[tk("bcs", eb), tk("nlam")], writes=[tk("bcs", eb)])
                    def s2():
                        S.op("dve", lambda e: e.tensor_tensor(out=Oc[eb][:], in0=Oc[eb][:], in1=bcs[eb][:], op=ALU.mult),
                             reads=[tk("Oc", eb), tk("bcs", eb)], writes=[tk("Oc", eb)])
                        S.op("dve", lambda e: e.tensor_tensor(out=Oc[eb][:, 0, :], in0=Oc[eb][:, 0, :], in1=Oc[eb][:, 1, :], op=ALU.add),
                             reads=[tk("Oc", eb)], writes=[tk("Oc", eb)])
                    def s2b():
                        S.op("act", lambda e: e.activation(out=sq[eb][:], in_=Oc[eb][:, 0, :], func=AF.Square), reads=[tk("Oc", eb)], writes=[tk("sq", eb)])
                    def s3():
                        sl = SI[0] % 2
                        S.op("pe", lambda e: e.matmul(Sps[:, 0, sl, :], lhsT=ones_b[:], rhs=sq[eb][:], start=True, stop=True), reads=[tk("sq", eb), tk("ones_b")], writes=[tk("Sps", sl)])
                        S.op("act", lambda e: e.activation(out=rr[eb][:], in_=Sps[:, 0, sl, :], func=AF.Ln, scale=1.0 / 128.0, bias=eps5[:]), reads=[tk("Sps", sl)], writes=[tk("rr", eb)])
                        S.op("act", lambda e: e.activation(out=rr[eb][:], in_=rr[eb][:], func=AF.Exp, scale=-0.5), reads=[tk("rr", eb)], writes=[tk("rr", eb)])
                    def s4():
                        S.op("dve", lambda e: e.scalar_tensor_tensor(out=OTo[eb][:], in0=Oc[eb][:, 0, :], scalar=g8col[:], in1=rr[eb][:], op0=ALU.mult, op1=ALU.mult),
                             reads=[tk("Oc", eb), tk("rr", eb), tk("g8col")], writes=[tk("OTo", eb)])
                        S.dma("pool", OT_s[h, :, t0:t0 + QB], OTo[eb][:], co[eb], reads=[tk("OTo", eb)], writes=[tk("OT_s", h)])
                    return [s0, s1, s2, s2b, s3, s4]

                def load_q(bi):
                    hh, QQ = bi // NQ, bi % NQ
                    S.dma("sp", QTq[bi % 2][:], QT_s[hh, :, QQ * QB:(QQ + 1) * QB], cq[bi % 2], writes=[tk("QTq", bi % 2)])
                load_q(0)
                for g in range(NG):
                    load_kv(0, g)
                for h in range(NH):
                    hs = h % 2
                    for Q in range(NQ):
                        nj = 4 * Q + 4
                        q0 = Q * QB
                        eb = bidx % 2
                        qs = bidx % 2
                        if bidx + 1 < NH * NQ:
                            load_q(bidx + 1)
                        bidx += 1

                        def emit_S(j, sl):
                            cs = 128 * max(0, j - 4 * Q)

                            def f(e, j=j, sl=sl, qs=qs, cs=cs):
                                e.matmul(Sps[:, 0, sl, cs:QB], lhsT=K0p[:, j * 128:(j + 1) * 128], rhs=QTq[qs][:, cs:QB], start=True, stop=True)
                                return e.matmul(Sps[:, 1, sl, cs:QB], lhsT=K1p[:, j * 128:(j + 1) * 128], rhs=QTq[qs][:, cs:QB], start=True, stop=True)
                            S.op("pe", f, reads=[tk("K", j // 4), tk("Kz"), tk("QTq", qs)], writes=[tk("Sps", sl)])
                        slots = {}
                        slots[0] = SI[0] % 2; SI[0] += 1
                        emit_S(0, slots[0])
                        for j in range(nj):
                            if j + 1 < nj:
                                slots[j + 1] = SI[0] % 2; SI[0] += 1
                                emit_S(j + 1, slots[j + 1])
                            sl = slots[j]
                            ps_ = pidx % NPT; pidx += 1
                            jj = j - 4 * Q
                            c0 = 128 * jj if jj > 0 else 0
                            S.op("act", lambda e, sl=sl, ps_=ps_, c0=c0: e.activation(out=PT[ps_][:, :, c0:QB], in_=Sps[:, :, sl, c0:QB], func=AF.Exp, scale=0.125),
                                 reads=[tk("Sps", sl)], writes=[tk("PT", ps_)])
                            if jj >= 0:
                                S.op("pool", lambda e, ps_=ps_, c0=c0: e.memset(PT[ps_][64:128, :, c0:c0 + 64], 0.0), reads=[], writes=[tk("PT", ps_)])

                            for c in range(2):
                                S.op("pe", lambda e, j=j, ps_=ps_, c0=c0, nj=nj, c=c: e.matmul(OTp[:, c, c0:QB], lhsT=V_h[:, j, 0:128], rhs=PT[ps_][:, c, c0:QB],
                                                                                                  start=(j == 0), stop=(j == nj - 1), skip_group_check=True),
                                     reads=[tk("PT", ps_), tk("V", j // 4)], writes=[tk("OTp", c)])
                            for c in range(2):
                                S.op("pe", lambda e, j=j, ps_=ps_, c0=c0, nj=nj, c=c: e.matmul(denp[:, c, c0:QB], lhsT=ones_b[:], rhs=PT[ps_][:, c, c0:QB],
                                                                                                  start=(j == 0), stop=(j == nj - 1), skip_group_check=True),
                                     reads=[tk("PT", ps_), tk("ones_b")], writes=[tk("denp", c)])
                            if Q == NQ - 1 and j % 4 == 3 and h + 1 < NH:
                                load_kv(h + 1, j // 4)
                            GJ[0] += 1
                            release(False)
                        release(True)
                        for ent in [x for x in pending if x["eb"] == eb]:
                            pending.remove(ent); ent["fn"]()
                        ep = make_epilogue(h, Q, eb)
                        ep[0]()
                        bid = bidx
                        for k, (off, fn) in enumerate(zip((1, 6, 9, 40, 42), ep[1:])):
                            pending.append({"rel": GJ[0] + off, "eb": eb, "fn": fn, "k": k, "bid": bid})
                while pending:
                    pending.pop(0)["fn"]()
                S.flush()

        def mlp_phase(tag, xin_d, ZTs, g_row, xout_d, final, W2b, pre_hook=None):
            with ExitStack() as st:
                NB = NT // 256
                gfb = sbt(st, "gfb", [128, D], F32) if final else None
                ss = [sbt(st, f"ss{i}", [128, 1], F32) for i in range(2)]
                rstd = [sbt(st, f"rstd{i}", [128, 1], F32) for i in range(2)]
                xb = [sbt(st, f"xb{i}", [128, D], BF16) for i in range(2)]
                xn1 = sbt(st, "xn", [128, D], F32) if final else None; xn = [xn1, xn1]
                tp = [pst(st, f"tp{i}", [128, 8, 128], BF16) for i in range(2)]
                xt = [sbt(st, f"xt{i}", [128, 2, D], F32) for i in range(2)]
                zT1 = sbt(st, "zT", [128, 8, 256], BF16); zT = [zT1, zT1]
                xT = [sbt(st, f"xT{i}", [128, 8, 256], BF16) for i in range(2)]
                hidT = sbt(st, "hidT", [128, 32, 256], BF16)
                NR = 2
                rt = [sbt(st, f"rt{i}", [128, 256], F32) for i in range(NR)]
                ss2 = sbt(st, "ss2", [128, 1], F32); rs2 = sbt(st, "rs2", [128, 1], F32)
                yps = [pst(st, f"yps{i}", [128, D], F32) for i in range(2)]
                zps = [pst(st, f"zps{i}", [128, 512], F32) for i in range(2)]
                cg = S.chan(tag + "_g"); cgf = S.chan(tag + "_gf")
                cx = [S.chan(tag + "_x0"), S.chan(tag + "_x1")]; cz = [S.chan(tag + "_z0"), S.chan(tag + "_z1")]
                cso = [S.chan(tag + "_o0"), S.chan(tag + "_o1")]
                if pre_hook is not None:
                    pre_hook()
                if final:
                    S.dma("sp", gfb[:], gvec_d[4:5, :].partition_broadcast(128), cgf, writes=[tk("gfb")])
                W1r = [tk("W1", c) for c in range(8)]; Wzr = [tk("Wz", c) for c in range(8)]; W2r = [tk("W2", c) for c in range(8)]
                cnt = {"yi": 0, "zi": 0, "ri": 0}

                def load_x(b):
                    s = b % 2; t0 = b * 256
                    S.dma("sp", xt[s][:], xin_d[t0:t0 + 256, :].rearrange("(t p) d -> p t d", p=128), cx[s], writes=[tk("xt", s, 0), tk("xt", s, 1)])

                def load_z(b):
                    s = b % 2; t0 = b * 256
                    S.dma("sp", zT[s][:], ZTs[:, :, t0:t0 + 256].rearrange("c p t -> p c t"), cz[0], writes=[tk("zT")])

                def pre_y0(b):
                    s = b % 2
                    for t in range(2):
                        ys = cnt["yi"] % 2; cnt["yi"] += 1

                        def mm0(e, s=s, t=t, ys=ys):
                            for n in range(2):
                                for c in range(8):
                                    r = e.matmul(yps[ys][:, n * 512:(n + 1) * 512], lhsT=zT[s][:, c, t * 128:(t + 1) * 128], rhs=Wzb[:, c, n * 512:(n + 1) * 512],
                                                 start=(c == 0), stop=(c == 7))
                            return r
                        S.op("pe", mm0, reads=[tk("zT")] + Wzr, writes=[tk("yps", ys)])
                        S.op("dve", lambda e, s=s, t=t, ys=ys: e.tensor_tensor(out=xt[s][:, t, :], in0=xt[s][:, t, :], in1=yps[ys][:], op=ALU.add),
                             reads=[tk("yps", ys), tk("xt", s, t)], writes=[tk("xt", s, t)])

                def pre_norm(b):
                    s = b % 2
                    for t in range(2):
                        S.op("act", lambda e, s=s, t=t: e.activation(out=xb[t][:], in_=xt[s][:, t, :], func=AF.Square, accum_out=ss[t][:]),
                             reads=[tk("xt", s, t)], writes=[tk("xb", t), tk("ss", t)])
                        rstd_ops(ss[t][:], rstd[t][:], float(D), eps6, [tk("ss", t)], [tk("rstd", t)])
                        S.op("act", lambda e, s=s, t=t: e.activation(out=xb[t][:], in_=xt[s][:, t, :], func=AF.Copy, scale=rstd[t][:]),
                             reads=[tk("xt", s, t), tk("rstd", t)], writes=[tk("xb", t)])

                def pre_mult(b):
                    pass

                def pre_tr(b):
                    for t in range(2):
                        def tr(e, t=t):
                            for c in range(8):
                                i = e.transpose(out=tp[t][:, c, :], in_=xb[t][:, c * 128:(c + 1) * 128], identity=ident[:])
                            return i
                        S.op("pe", tr, reads=[tk("xb", t)], writes=[tk("tp", t)])

                def pre_copy(b):
                    for t in range(2):
                        S.op("dve", lambda e, t=t, b=b: e.tensor_tensor(out=xT[b % 2][:, :, t * 128:(t + 1) * 128], in0=tp[t][:], in1=gcol_bc(g_row), op=ALU.mult),
                             reads=[tk("tp", t)], writes=[tk("xT", b % 2)])

                def mm1_range(b, f0, f1):
                    for f in range(f0, f1):
                        zs = cnt["zi"] % 2; cnt["zi"] += 1
                        zp = zps[zs][:, 0:256]

                        def mm1(e, f=f, zp=zp, b=b):
                            for c in range(8):
                                r = e.matmul(zp, lhsT=W1b[:, c, f * 128:(f + 1) * 128], rhs=xT[b % 2][:, c, :], start=(c == 0), stop=(c == 7))
                            return r
                        S.op("pe", mm1, reads=[tk("xT", b % 2)] + W1r, writes=[tk("zps", zs)])
                        r_ = cnt["ri"] % NR; cnt["ri"] += 1
                        S.op("act", lambda e, zp=zp, r_=r_: e.activation(out=rt[r_][:], in_=zp, func=AF.Relu), reads=[tk("zps", zs)], writes=[tk("rt", r_)])
                        S.op("pool", lambda e, f=f, r_=r_: e.tensor_tensor(out=hidT[:, f, :], in0=rt[r_][:], in1=rt[r_][:], op=ALU.mult),
                             reads=[tk("rt", r_)], writes=[tk("hidT", f)])

                def mm2_t(b, t):
                    s = b % 2
                    ys = cnt["yi"] % 2; cnt["yi"] += 1
                    for n in range(2):
                        for fg in range(4):
                            def mm2(e, t=t, ys=ys, n=n, fg=fg):
                                for f in range(fg * 8, fg * 8 + 8):
                                    r = e.matmul(yps[ys][:, n * 512:(n + 1) * 512], lhsT=hidT[:, f, t * 128:(t + 1) * 128], rhs=W2b[:, f, n * 512:(n + 1) * 512],
                                                 start=(f == 0), stop=(f == 31))
                                return r
                            S.op("pe", mm2, reads=[tk("hidT", f) for f in range(fg * 8, fg * 8 + 8)] + W2r, writes=[tk("yps", ys)])
                    S.op("dve", lambda e, s=s, t=t, ys=ys: e.tensor_tensor(out=xt[s][:, t, :], in0=xt[s][:, t, :], in1=yps[ys][:], op=ALU.add),
                         reads=[tk("yps", ys), tk("xt", s, t)], writes=[tk("xt", s, t)])
                    if final:
                        S.op("act", lambda e, s=s, t=t: e.activation(out=xn[t][:], in_=xt[s][:, t, :], func=AF.Square, accum_out=ss2[:]),
                             reads=[tk("xt", s, t)], writes=[tk("xn"), tk("ss2")])
                        rstd_ops(ss2[:], rs2[:], float(D), eps6, [tk("ss2")], [tk("rs2")])
                        S.op("dve", lambda e, s=s, t=t: e.scalar_tensor_tensor(out=xt[s][:, t, :], in0=xt[s][:, t, :], scalar=rs2[:], in1=gfb[:], op0=ALU.mult, op1=ALU.mult),
                             reads=[tk("xt", s, t), tk("rs2"), tk("gfb")], writes=[tk("xt", s, t)])

                def store(b):
                    s = b % 2; t0 = b * 256
                    S.dma("sp", xout_d[t0:t0 + 256, :].rearrange("(t p) d -> p t d", p=128), xt[s][:], cso[s], reads=[tk("xt", s, 0), tk("xt", s, 1)], writes=[tk("xout", b)])

                load_x(0); load_z(0)
                if NB > 1:
                    load_x(1)
                pre_y0(0)
                if NB > 1:
                    load_z(1)
                pre_norm(0); pre_mult(0); pre_tr(0); pre_copy(0)
                for b in range(NB):
                    nxt = b + 1 < NB
                    mm1_range(b, 0, 8)
                    if nxt:
                        pre_y0(b + 1)
                        if b + 2 < NB:
                            load_z(b + 2)
                    mm1_range(b, 8, 16)
                    if nxt:
                        pre_norm(b + 1)
                    mm1_range(b, 16, 24)
                    if nxt:
                        pre_mult(b + 1)
                    mm1_range(b, 24, 32)
                    if nxt:
                        pre_tr(b + 1)
                    mm2_t(b, 0)
                    if nxt:
                        pre_copy(b + 1)
                    mm2_t(b, 1)
                    store(b)
                    if b + 2 < NB:
                        load_x(b + 2)
                S.flush()

        if "C" in phases:
            hook = None if "B" in phases else (lambda: (load_w1z(w1_d[0], woa_d), load_w2(w2_d[0])))
            mlp_phase("C", x_d, OT_s, 1, x2_s, False, W2box[0], pre_hook=hook)
        sc2.close()

        if "D" in phases:
            with ExitStack() as st:
                TB = 512 if NT >= 512 else NT
                NBK = NT // TB
                NTB = TB // 128
                Wxb = sbt(st, "Wxb", [128, 8, D], BF16); Wyb = sbt(st, "Wyb", [128, 8, D], BF16)
                Wab = sbt(st, "Wab", [128, 4, 2, 256], BF16); Wib = sbt(st, "Wib", [128, 4, 2, 256], BF16)
                gbc = None
                PB = mk_prologue_bufs(st, 1)
                xt = [sbt(st, f"xt{i}", [128, D], F32) for i in range(2)]
                xT = [sbt(st, f"xT{i}", [128, 8, TB], BF16) for i in range(2)]
                xpre = [sbt(st, f"xpre{i}", [128, TB + 3], F32) for i in range(4)]
                halo = sbt(st, "halo", [128, 8, 4], F32); one1 = sbt(st, "one1", [128, 1], F32)
                xc = [sbt(st, f"xc{i}", [128, TB], F32) for i in range(4)]
                xcb = [sbt(st, f"xcb{i}", [128, TB], BF16) for i in range(4)]
                gate = [sbt(st, f"gate{i}", [128, TB], BF16) for i in range(4)]
                g1 = [sbt(st, f"g1{i}", [128, TB], F32) for i in range(2)]
                ga = [sbt(st, f"ga{i}", [128, TB], F32) for i in range(2)]; gi_ = [sbt(st, f"gi{i}", [128, TB], F32) for i in range(2)]
                aa = [sbt(st, f"aa{i}", [128, TB], F32) for i in range(2)]; m2 = [sbt(st, f"m2{i}", [128, TB], F32) for i in range(2)]
                hs_ = [sbt(st, f"hs{i}", [128, TB], F32) for i in range(2)]
                yT_acc = [sbt(st, "yTa0", [128, 8, TB], BF16)]
                hlast = sbt(st, "hlast", [128, 8], F32)
                nb = sbt(st, "nb", [128, 16], F32)
                sp8 = sbt(st, "sp8", [128, 8], F32); spe = sbt(st, "spe", [128, 8], F32); spt = sbt(st, "spt", [128, 8], F32)
                psx1 = pst(st, "psx0", [128, 512], F32); psx = [psx1, psx1]
                psy = [pst(st, f"psy{i}", [128, 512], F32) for i in range(2)]
                psa = [pst(st, f"psa{i}", [128, 512], F32) for i in range(2)]; psi = [pst(st, f"psi{i}", [128, 512], F32) for i in range(2)]
                cw = [S.chan("D_wx"), S.chan("D_wy"), S.chan("D_wa"), S.chan("D_wi")]
                cg = S.chan("D_g"); cx = [S.chan("D_x0"), S.chan("D_x1")]; cso = [S.chan("D_o0"), S.chan("D_o1")]
                for n in range(4):
                    S.dma("pool", Wxb[:, :, n * 256:(n + 1) * 256], wx_d[:, n * 256:(n + 1) * 256].rearrange("(c p) f -> p c f", p=128), cw[n], writes=[tk("Wx", n)])
                    S.dma("pool", Wyb[:, :, n * 256:(n + 1) * 256], wy_d[:, n * 256:(n + 1) * 256].rearrange("(c p) f -> p c f", p=128), cw[n], writes=[tk("Wy", n)])
                    S.dma("pool", Wab[:, n, :, :], wa_d[n].rearrange("(i p) o -> p i o", p=128), cw[n], writes=[tk("Wa", n)])
                    S.dma("pool", Wib[:, n, :, :], wi_d[n].rearrange("(i p) o -> p i o", p=128), cw[n], writes=[tk("Wi", n)])
                if "E" in phases:
                    load_w1z(w1_d[1], wor_d)
                S.op("dve", lambda e: e.tensor_scalar(out=nb[:], in0=cvec[:, 40:56], scalar1=-1.0, scalar2=None, op0=ALU.mult), writes=[tk("nb")])
                S.op("act", lambda e: e.activation(out=spe[:], in_=cvec[:, 56:64], func=AF.Exp, scale=-1.0), writes=[tk("spe")])
                S.op("dve", lambda e: e.tensor_scalar(out=spt[:], in0=spe[:], scalar1=1.0 / 3.0, scalar2=-0.5, op0=ALU.mult, op1=ALU.add), reads=[tk("spe")], writes=[tk("spt")])
                S.op("dve", lambda e: e.tensor_tensor(out=spt[:], in0=spt[:], in1=spe[:], op=ALU.mult), reads=[tk("spt"), tk("spe")], writes=[tk("spt")])
                S.op("dve", lambda e: e.tensor_scalar(out=spt[:], in0=spt[:], scalar1=1.0, scalar2=None, op0=ALU.add), reads=[tk("spt")], writes=[tk("spt")])
                S.op("dve", lambda e: e.tensor_tensor(out=spt[:], in0=spt[:], in1=spe[:], op=ALU.mult), reads=[tk("spt"), tk("spe")], writes=[tk("spt")])
                S.op("dve", lambda e: e.tensor_scalar(out=sp8[:], in0=spt[:], scalar1=-8.0, scalar2=None, op0=ALU.mult), reads=[tk("spt")], writes=[tk("sp8")])
                S.op("dve", lambda e: e.memset(hlast[:], 0.0), writes=[tk("hlast")])
                S.op("pool", lambda e: e.memset(halo[:], 0.0), writes=[tk("halo", c) for c in range(8)])
                S.op("pool", lambda e: e.memset(one1[:], 1.0), writes=[tk("one1")])
                XI = [0]

                def pro(blk):
                    for t in range(NTB):
                        s = XI[0] % 2; XI[0] += 1
                        i = blk * NTB + t
                        S.dma("sp", xt[s][:], x2_s[i * 128:(i + 1) * 128, :], cx[s], reads=[tk("x2")], writes=[tk("xt", s)])
                        tp, tptok = prologue(PB, tk("xt", s), xt[s][:], None, gbc)
                        S.op("dve", lambda e, t=t, tp=tp, blk=blk: e.tensor_tensor(out=xT[blk % 2][:, :, t * 128:(t + 1) * 128], in0=tp[:], in1=gcol_bc(2), op=ALU.mult),
                             reads=[tptok], writes=[tk("xT", blk % 2)])

                def stA(g):
                    blk, n = g // 4, g % 4
                    pair = (2 * n, 2 * n + 1)
                    for cc in pair:
                        xs = cc % 2
                        for (ps_, W_, nm, xk) in ((psx[xs], Wxb, "psx", 0), (psy[xs], Wyb, "psy", xs)):
                            def mmxy(e, ps_=ps_, W_=W_, cc=cc, blk=blk):
                                for c in range(8):
                                    r = e.matmul(ps_[:, 0:TB], lhsT=W_[:, c, cc * 128:(cc + 1) * 128], rhs=xT[blk % 2][:, c, :], start=(c == 0), stop=(c == 7))
                                return r
                            S.op("pe", mmxy, reads=[tk("xT", blk % 2), tk("Wx", cc // 2), tk("Wy", cc // 2)], writes=[tk(nm, xk)])
                        q4 = cc % 4
                        S.op("dve", lambda e, cc=cc, q4=q4: e.tensor_copy(out=xpre[q4][:, 0:3], in_=halo[:, cc, 0:3]), reads=[tk("halo", cc)], writes=[tk("xpre", q4)])
                        S.op("act", lambda e, q4=q4, xs=xs: e.activation(out=xpre[q4][:, 3:TB + 3], in_=psx[xs][:, 0:TB], func=AF.Copy),
                             reads=[tk("psx", 0)], writes=[tk("xpre", q4)])
                        S.op("dve", lambda e, cc=cc, q4=q4: e.tensor_copy(out=halo[:, cc, 0:3], in_=xpre[q4][:, TB:TB + 3]), reads=[tk("xpre", q4)], writes=[tk("halo", cc)])

                def stB(g):
                    blk, n = g // 4, g % 4
                    pair = (2 * n, 2 * n + 1)
                    for cc in pair:
                        xs = cc % 2; q4 = cc % 4
                        S.op("act", lambda e, xs=xs: e.activation(out=g1[xs][:], in_=psy[xs][:, 0:TB], func=AF.Square), reads=[tk("psy", xs)], writes=[tk("g1", xs)])

                def stC(g):
                    blk, n = g // 4, g % 4
                    pair = (2 * n, 2 * n + 1)
                    for cc in pair:
                        q4 = cc % 4
                        S.op("pool", lambda e, cc=cc, q4=q4: e.tensor_scalar(out=xc[q4][:], in0=xpre[q4][:, 3:TB + 3], scalar1=cvec[:, 24 + cc:25 + cc], scalar2=cvec[:, 32 + cc:33 + cc],
                                                                              op0=ALU.mult, op1=ALU.add), reads=[tk("xpre", q4)], writes=[tk("xc", q4)])
                        for j in (2, 1, 0):
                            S.op("dve", lambda e, cc=cc, q4=q4, j=j: e.scalar_tensor_tensor(out=xc[q4][:], in0=xpre[q4][:, j:TB + j], scalar=cvec[:, j * 8 + cc:j * 8 + cc + 1], in1=xc[q4][:],
                                                                                             op0=ALU.mult, op1=ALU.add), reads=[tk("xpre", q4), tk("xc", q4)], writes=[tk("xc", q4)])
                        S.op("act", lambda e, q4=q4: e.activation(out=xcb[q4][:], in_=xc[q4][:], func=AF.Copy), reads=[tk("xc", q4)], writes=[tk("xcb", q4)])

                def stD(g):
                    blk, n = g // 4, g % 4
                    pair = (2 * n, 2 * n + 1)
                    for cc in pair:
                        xs = cc % 2
                        S.op("dve", lambda e, xs=xs: e.tensor_scalar(out=g1[xs][:], in0=g1[xs][:], scalar1=GC, scalar2=1.0, op0=ALU.mult, op1=ALU.add),
                             reads=[tk("g1", xs)], writes=[tk("g1", xs)])
                        S.op("dve", lambda e, xs=xs: e.tensor_tensor(out=g1[xs][:], in0=g1[xs][:], in1=psy[xs][:, 0:TB], op=ALU.mult),
                             reads=[tk("g1", xs), tk("psy", xs)], writes=[tk("g1", xs)])
                        S.op("act", lambda e, xs=xs: e.activation(out=g1[xs][:], in_=g1[xs][:], func=AF.Sigmoid, scale=GK), reads=[tk("g1", xs)], writes=[tk("g1", xs)])
                        S.op("dve", lambda e, xs=xs, cc=cc: e.tensor_tensor(out=gate[cc % 4][:], in0=g1[xs][:], in1=psy[xs][:, 0:TB], op=ALU.mult),
                             reads=[tk("g1", xs), tk("psy", xs)], writes=[tk("gate", cc % 4)])

                def stE(g):
                    blk, n = g // 4, g % 4
                    pair = (2 * n, 2 * n + 1)
                    for oc in pair:
                        ol = oc % 2
                        for (pg, Wg, nm) in ((psa[ol], Wab, "psa"), (psi[ol], Wib, "psi")):
                            def mmg(e, pg=pg, Wg=Wg, n=n, ol=ol):
                                for l in range(2):
                                    r = e.matmul(pg[:, 0:TB], lhsT=Wg[:, n, l, ol * 128:(ol + 1) * 128], rhs=xcb[(2 * n + l) % 4][:], start=(l == 0), stop=(l == 1))
                                return r
                            S.op("pe", mmg, reads=[tk("xcb", (2 * n) % 4), tk("xcb", (2 * n + 1) % 4), tk("Wa", n), tk("Wi", n)], writes=[tk(nm, ol)])
                        S.op("act", lambda e, oc=oc, ol=ol: e.activation(out=ga[ol][:], in_=psa[ol][:, 0:TB], func=AF.Sigmoid, bias=cvec[:, 40 + oc:41 + oc]),
                             reads=[tk("psa", ol)], writes=[tk("ga", ol)])
                        S.op("act", lambda e, oc=oc, ol=ol: e.activation(out=gi_[ol][:], in_=psi[ol][:, 0:TB], func=AF.Sigmoid, bias=cvec[:, 48 + oc:49 + oc]),
                             reads=[tk("psi", ol)], writes=[tk("gi", ol)])
                        S.op("pool", lambda e, ol=ol, oc=oc: e.tensor_tensor(out=gi_[ol][:], in0=gi_[ol][:], in1=xc[oc % 4][:], op=ALU.mult), reads=[tk("gi", ol), tk("xc", oc % 4)], writes=[tk("gi", ol)])

                def stFG(g):
                    blk, n = g // 4, g % 4
                    pair = (2 * n, 2 * n + 1)
                    for oc in pair:
                        ol = oc % 2
                        S.op("act", lambda e, oc=oc, ol=ol: e.activation(out=aa[ol][:], in_=ga[ol][:], func=AF.Exp, scale=sp8[:, oc:oc + 1]), reads=[tk("ga", ol), tk("sp8")], writes=[tk("aa", ol)])
                    for oc in pair:
                        ol = oc % 2
                        S.op("act", lambda e, ol=ol: e.activation(out=m2[ol][:], in_=aa[ol][:], func=AF.Square), reads=[tk("aa", ol)], writes=[tk("m2", ol)])
                    for oc in pair:
                        ol = oc % 2
                        S.op("act", lambda e, ol=ol: e.activation(out=m2[ol][:], in_=m2[ol][:], func=AF.Ln, scale=-1.0, bias=one1[:]), reads=[tk("m2", ol)], writes=[tk("m2", ol)])
                    for oc in pair:
                        ol = oc % 2
                        S.op("act", lambda e, ol=ol: e.activation(out=m2[ol][:], in_=m2[ol][:], func=AF.Exp, scale=0.5), reads=[tk("m2", ol)], writes=[tk("m2", ol)])

                def stH(g):
                    blk, n = g // 4, g % 4
                    pair = (2 * n, 2 * n + 1)
                    for oc in pair:
                        ol = oc % 2; o4 = oc % 4
                        S.op("dve", lambda e, ol=ol: e.tensor_tensor(out=gi_[ol][:], in0=gi_[ol][:], in1=m2[ol][:], op=ALU.mult), reads=[tk("gi", ol), tk("m2", ol)], writes=[tk("gi", ol)])
                        S.op("dve", lambda e, oc=oc, ol=ol: e.tensor_tensor_scan(out=hs_[ol][:], data0=aa[ol][:], data1=gi_[ol][:], initial=hlast[:, oc:oc + 1], op0=ALU.mult, op1=ALU.add),
                             reads=[tk("aa", ol), tk("gi", ol), tk("hlast")], writes=[tk("hs", ol)])
                        S.op("dve", lambda e, oc=oc, ol=ol: e.tensor_copy(out=hlast[:, oc:oc + 1], in_=hs_[ol][:, TB - 1:TB]), reads=[tk("hs", ol)], writes=[tk("hlast")])
                        S.op("pool", lambda e, oc=oc, ol=ol, o4=o4: e.tensor_tensor(out=yT_acc[0][:, oc, :], in0=hs_[ol][:], in1=gate[o4][:], op=ALU.mult),
                             reads=[tk("hs", ol), tk("gate", o4)], writes=[tk("yTa", 0)])

                G = NBK * 4
                pro(0); stA(0); stB(0); stC(0); stD(0)
                for g in range(G):
                    stE(g)
                    if g % 4 == 1 and g // 4 + 1 < NBK:
                        pro(g // 4 + 1)
                    if g + 1 < G:
                        stA(g + 1); stB(g + 1)
                    stFG(g)
                    if g + 1 < G:
                        stC(g + 1)
                        stD(g + 1)
                    stH(g)
                    if g % 4 == 3:
                        blk = g // 4
                        S.dma("sp", ZT_s[:, :, blk * TB:(blk + 1) * TB].rearrange("c p t -> p c t"), yT_acc[0][:], cso[0], reads=[tk("yTa", 0)], writes=[tk("ZT_s")])
                S.flush()

        if "E" in phases:
            sc2b = ExitStack()
            W2box[0] = sbt(sc2b, "W2b_e", [128, 32, D], BF16)
            mlp_phase("E", x2_s, ZT_s, 3, out_d, True, W2box[0], pre_hook=lambda: load_w2(w2_d[1]))
            sc2b.close()
        sc1.close()
    return nc


def _prep_inputs(inputs, NT=4096):
    f = lambda a: np.ascontiguousarray(np.asarray(a, dtype=np.float32))
    x = f(inputs["x"])
    B = x.shape[0]
    shared = {
        "wqkv": f(inputs["attn_w_qkv"][0]), "woa": f(inputs["attn_w_o"][0]),
        "w1_0": f(inputs["mlp_w1"][0]), "w1_1": f(inputs["mlp_w1"][1]),
        "w2_0": f(inputs["mlp_w2"][0]), "w2_1": f(inputs["mlp_w2"][1]),
        "wx": f(inputs["rec_w_x"][0]), "wy": f(inputs["rec_w_y"][0]), "wor": f(inputs["rec_w_o"][0]),
        "wa": f(inputs["rec_w_a"][0]), "wi": f(inputs["rec_w_i"][0]),
    }
    gvec = np.stack([f(inputs["mix_norm_g"])[0], f(inputs["mlp_norm_g"])[0], f(inputs["mix_norm_g"])[1],
                     f(inputs["mlp_norm_g"])[1], f(inputs["final_norm_g"])], axis=0)
    shared["gvec"] = np.ascontiguousarray(gvec)
    pc = lambda v: np.asarray(v, np.float32).reshape(8, 128).T
    cw = f(inputs["rec_conv_w"][0])
    cvec = np.concatenate([pc(cw[0]), pc(cw[1]), pc(cw[2]), pc(cw[3]), pc(inputs["rec_conv_b"][0]), pc(inputs["rec_b_a"][0]),
                           pc(inputs["rec_b_i"][0]), pc(inputs["rec_lambda"][0]),
                           pc(gvec[0]), pc(gvec[1]), pc(gvec[2]), pc(gvec[3])], axis=1)
    shared["cvec"] = np.ascontiguousarray(cvec.astype(np.float32))
    shared["lamv"] = np.ascontiguousarray(np.concatenate([f(inputs["attn_lq1"][0]), f(inputs["attn_lk1"][0]),
                                                          f(inputs["attn_lq2"][0]), f(inputs["attn_lk2"][0])])[None, :])
    shared["subg"] = f(inputs["attn_subln_g"][0])[None, :]
    inv_freq = (1.0 / (np.float32(10000.0) ** (np.arange(0, 64, 2, dtype=np.float32) / np.float32(64.0)))).astype(np.float32)
    ang = np.arange(NT, dtype=np.float32)[:, None] * inv_freq[None, :]
    cos, sin = np.cos(ang).astype(np.float32), np.sin(ang).astype(np.float32)
    shared["rope"] = np.ascontiguousarray(np.concatenate([cos, cos, -sin, sin], axis=1).astype(np.float32))
    shared["ident"] = np.eye(128, dtype=np.float32)
    in_maps = []
    for b in range(B):
        m = dict(shared)
        m["x"] = np.ascontiguousarray(x[b, :NT])
        in_maps.append(m)
    return in_maps


_NC_CACHE = {}


def kernel(**inputs):
    x = np.asarray(inputs["x"])
    B, NT, _ = x.shape
    if NT not in _NC_CACHE:
        _NC_CACHE[NT] = build_nc(NT)
    nc = _NC_CACHE[NT]
    in_maps = _prep_inputs(inputs, NT)
    res = run_bass_kernel_spmd(nc, in_maps, core_ids=list(range(B)))
    out = np.stack([np.asarray(r["out"]).reshape(NT, D) for r in res.results], axis=0)
    return out.astype(np.float32)
```

```python
import numpy as np
from contextlib import ExitStack
import concourse.bass as bass
import concourse.mybir as mybir
from concourse.bass_utils import run_bass_kernel_spmd

F32 = mybir.dt.float32
BF16 = mybir.dt.bfloat16
AF = mybir.ActivationFunctionType
ALU = mybir.AluOpType
AX = mybir.AxisListType


class Tok:
    __slots__ = ("name", "last_w", "readers")

    def __init__(self, name):
        self.name = name
        self.last_w = None
        self.readers = []


class Chan:
    def __init__(self, sem, name):
        self.sem = sem
        self.name = name
        self.count = 0


class Op:
    __slots__ = ("eng", "fn", "reads", "writes", "chan", "deps", "sig", "sigval", "idx")


class Sched:
    ENGS = ("sp", "act", "dve", "pool", "pe")

    def __init__(self, nc, stack):
        self.nc = nc
        self.stack = stack
        self.e = {"sp": nc.sync, "act": nc.scalar, "dve": nc.vector, "pool": nc.gpsimd, "pe": nc.tensor}
        self.prog = {k: stack.enter_context(nc.semaphore("prog_" + k)) for k in self.ENGS if k != "sp"}
        self.sigcount = {k: 0 for k in self.ENGS}
        self.waited = {k: {} for k in self.ENGS}
        self.ops = []
        self.chans = []
        self.nsem = 4

    def chan(self, name):
        c = Chan(self.stack.enter_context(self.nc.semaphore("ch_" + name)), name)
        self.chans.append(c)
        self.nsem += 1
        return c

    def op(self, eng, fn, reads=(), writes=(), chan=None):
        o = Op()
        o.eng, o.fn, o.reads, o.writes, o.chan = eng, fn, tuple(reads), tuple(writes), chan
        o.deps, o.sig, o.sigval = [], False, None
        self.ops.append(o)
        return o

    def dma(self, eng, out, in_, chan, reads=(), writes=()):
        return self.op(eng, lambda e: e.dma_start(out=out, in_=in_), reads, writes, chan)

    def flush(self, barrier=True):
        ops = self.ops
        for k, o in enumerate(ops):
            o.idx = k
            deps = set()
            for r in o.reads:
                if r.last_w is not None:
                    deps.add(r.last_w)
            for w in o.writes:
                if w.last_w is not None:
                    deps.add(w.last_w)
                for rd in w.readers:
                    deps.add(rd)
            deps.discard(k)
            for d in sorted(deps):
                od = ops[d]
                if od.eng == o.eng == "pe" and od.chan is None and o.chan is None:
                    continue
                o.deps.append(od)
                if od.chan is None:
                    od.sig = True
            for r in o.reads:
                r.readers.append(k)
            for w in o.writes:
                w.last_w = k
                w.readers = []
        for o in ops:
            eng = self.e[o.eng]
            wt = self.waited[o.eng]
            for od in o.deps:
                if od.chan is not None:
                    sem, val = od.chan.sem, od.chan.count
                else:
                    sem, val = self.prog[od.eng], od.sigval
                assert val is not None
                key = id(sem)
                if wt.get(key, 0) >= val:
                    continue
                wt[key] = val
                eng.wait_ge(sem, val)
            inst = o.fn(eng)
            if o.chan is not None:
                o.chan.count += 16
                o.sigval = o.chan.count
                inst.then_inc(o.chan.sem, 16)
                o.chan.last_eng = o.eng
            elif o.sig:
                self.sigcount[o.eng] += 1
                o.sigval = self.sigcount[o.eng]
                inst.then_inc(self.prog[o.eng], 1)
        for c in self.chans:
            if c.count and getattr(c, "last_eng", None) is not None:
                wt = self.waited[c.last_eng]
                if wt.get(id(c.sem), 0) < c.count:
                    wt[id(c.sem)] = c.count
                    self.e[c.last_eng].wait_ge(c.sem, c.count)
        if barrier:
            self.nc.all_engine_barrier()
        seen = set()
        for o in ops:
            for t in o.reads + o.writes:
                if id(t) not in seen:
                    seen.add(id(t))
                    t.last_w = None
                    t.readers = []
        self.ops = []


class TK:
    def __init__(self):
        self.d = {}

    def __call__(self, *key):
        t = self.d.get(key)
        if t is None:
            t = self.d[key] = Tok(str(key))
        return t


D = 1024
DFF = 4096
NH = 8
EPS = 1e-6
SUB_EPS = 1e-5
LAMBDA_INIT = 0.8 - 0.6 * 1.0
GK = 0.7978845608028654 * 2.0
GC = 0.044715


def build_nc(NT=4096, debug=False, phases="ABCDE"):
    NTL = NT // 128
    nc = bass.Bass("TRN2", target_bir_lowering=False)
    dt_in = lambda n, s: nc.dram_tensor(n, list(s), F32, kind="ExternalInput").ap()
    skind = "ExternalOutput" if debug else "Internal"
    dt_s = lambda n, s, d: nc.dram_tensor(n, list(s), d, kind=skind).ap()
    x_d = dt_in("x", [NT, D])
    wqkv_d = dt_in("wqkv", [D, 3 * D]); woa_d = dt_in("woa", [D, D])
    w1_d = [dt_in("w1_0", [D, DFF]), dt_in("w1_1", [D, DFF])]
    w2_d = [dt_in("w2_0", [DFF, D]), dt_in("w2_1", [DFF, D])]
    wx_d = dt_in("wx", [D, D]); wy_d = dt_in("wy", [D, D]); wor_d = dt_in("wor", [D, D])
    wa_d = dt_in("wa", [4, 256, 256]); wi_d = dt_in("wi", [4, 256, 256])
    gvec_d = dt_in("gvec", [5, D])
    cvec_d = dt_in("cvec", [128, 96])
    lamv_d = dt_in("lamv", [1, 256]); subg_d = dt_in("subg", [1, 128])
    rope_d = dt_in("rope", [NT, 128]); ident_d = dt_in("ident", [128, 128])
    out_d = nc.dram_tensor("out", [NT, D], F32, kind="ExternalOutput").ap()
    QT_s = dt_s("QT_s", [NH, 128, NT], BF16); KT_s = dt_s("KT_s", [NH, 128, NT], BF16)
    V_s = dt_s("V_s", [NH, 128, NTL, 129], BF16)
    OT_s = dt_s("OT_s", [NH, 128, NT], BF16)
    x2_s = dt_s("x2_s", [NT, D], F32)
    ZT_s = dt_s("ZT_s", [8, 128, NT], BF16)

    with ExitStack() as top:
        S = Sched(nc, top)
        tk = TK()
        ucnt = [0]

        def sbt(st, n, s, d):
            ucnt[0] += 1
            return st.enter_context(nc.sbuf_tensor(f"s{ucnt[0]}_{n}", list(s), d))

        def pst(st, n, s, d):
            ucnt[0] += 1
            return st.enter_context(nc.psum_tensor(f"p{ucnt[0]}_{n}", list(s), d))
        ident = sbt(top, "ident", [128, 128], BF16)
        eps6 = sbt(top, "eps6", [128, 1], F32); eps5 = sbt(top, "eps5", [128, 1], F32)
        cvec = sbt(top, "cvec", [128, 96], F32)
        c_const = S.chan("const")
        c_const2 = S.chan("const2")
        S.dma("pool", ident[:], ident_d, c_const2, writes=[tk("ident")])
        S.dma("sp", cvec[:], cvec_d, c_const, writes=[tk("cvec")])
        S.op("dve", lambda e: e.memset(eps6[:], EPS), writes=[tk("eps6")])
        S.op("dve", lambda e: e.memset(eps5[:], SUB_EPS), writes=[tk("eps5")])
        S.flush()
        CONST_R = [tk("ident"), tk("eps6"), tk("eps5"), tk("cvec")]

        def rstd_ops(ss_ap, out_ap, n, eps_t, toks_r, toks_w):
            S.op("act", lambda e: e.activation(out=out_ap, in_=ss_ap, func=AF.Ln, scale=1.0 / n, bias=eps_t[:]), reads=toks_r, writes=toks_w)
            S.op("act", lambda e: e.activation(out=out_ap, in_=out_ap, func=AF.Exp, scale=-0.5), reads=toks_w, writes=toks_w)

        def gcol_bc(g_row, w=128):
            return cvec[:, 64 + 8 * g_row:72 + 8 * g_row].unsqueeze(2).to_broadcast([128, 8, w])

        def prologue_act(st_bufs, key, xt_ap):
            B = st_bufs
            s = B["pi"] % 2
            B["pi"] += 1
            ss, rstd, xb = B["ss"][s], B["rstd"][s], B["xb"][s]
            S.op("act", lambda e: e.activation(out=xb[:], in_=xt_ap, func=AF.Square, accum_out=ss[:]),
                 reads=[key], writes=[tk("xb", s), tk("ss", s)])
            rstd_ops(ss[:], rstd[:], float(D), eps6, [tk("ss", s)], [tk("rstd", s)])
            S.op("act", lambda e: e.activation(out=xb[:], in_=xt_ap, func=AF.Copy, scale=rstd[:]),
                 reads=[key, tk("rstd", s)], writes=[tk("xb", s)])
            return s

        def prologue_pe(st_bufs, s):
            B = st_bufs
            xb, tp = B["xb"][s], B["tp"][s % len(B["tp"])]
            tps = s % len(B["tp"])

            def tr(e):
                for c in range(8):
                    i = e.transpose(out=tp[:, c, :], in_=xb[:, c * 128:(c + 1) * 128], identity=ident[:])
                return i
            S.op("pe", tr, reads=[tk("xb", s)], writes=[tk("tp", tps)])
            return tp, tk("tp", tps)

        def prologue(st_bufs, key, xt_ap, xT_dst, gbc):
            s = prologue_act(st_bufs, key, xt_ap)
            return prologue_pe(st_bufs, s)

        def mk_prologue_bufs(st, ntp=1):
            B = {"pi": 0}
            B["ss"] = [sbt(st, f"ss{i}", [128, 1], F32) for i in range(2)]
            B["rstd"] = [sbt(st, f"rstd{i}", [128, 1], F32) for i in range(2)]
            B["xb"] = [sbt(st, f"xb{i}", [128, D], BF16) for i in range(2)]
            B["tp"] = [pst(st, f"tp{i}", [128, 8, 128], BF16) for i in range(ntp)]
            return B

        if "A" in phases:
            with ExitStack() as st:
                wq = sbt(st, "wqkv_b", [128, 8, 3 * D], BF16)
                gbc = None
                ropeT = sbt(st, "ropeT", [128, NTL, 128], F32)
                PB = mk_prologue_bufs(st, 1)
                xt = [sbt(st, f"xt{i}", [128, D], F32) for i in range(2)]
                xT = [sbt(st, f"xT{i}", [128, 8, 128], BF16) for i in range(2)]
                ksb = sbt(st, "ksb", [128, D], F32)
                rA = [sbt(st, f"rA{i}", [128, D], F32) for i in range(2)]
                rB = [sbt(st, f"rB{i}", [128, D], F32) for i in range(2)]
                qkb = [[sbt(st, f"qkb{w}{i}", [128, D], BF16) for i in range(2)] for w in range(2)]
                qT_acc = [sbt(st, f"qTa{i}", [128, 8, 512], BF16) for i in range(2)]
                kT_acc = [sbt(st, f"kTa{i}", [128, 8, 512], BF16) for i in range(2)]
                V_acc = [sbt(st, f"Va{i}", [128, 8, 4, 129], BF16) for i in range(2)]
                pq = pst(st, "pq", [128, D], F32); pk = pst(st, "pk", [128, D], F32); pv = pst(st, "pv", [128, D], F32)
                tq = pst(st, "tq", [128, 8, 128], BF16)
                cwq = [S.chan(f"A_w{i}") for i in range(3)]; cg = S.chan("A_g"); cx = [S.chan("A_x0"), S.chan("A_x1")]
                cst = [S.chan("A_st0"), S.chan("A_st1")]
                for cb6 in range(6):
                    S.dma("pool", wq[:, :, cb6 * 512:(cb6 + 1) * 512], wqkv_d[:, cb6 * 512:(cb6 + 1) * 512].rearrange("(c p) f -> p c f", p=128), cwq[cb6 // 2], writes=[tk("wq", cb6)])
                S.dma("sp", ropeT[:], rope_d.rearrange("(i p) f -> p i f", p=128), cg, writes=[tk("ropeT")])
                for i in range(2):
                    S.op("dve", lambda e, i=i: e.memset(V_acc[i][:, :, :, 128:129], 1.0), writes=[tk("Vacc", i)])

                def rope(eng, src_ap, src_tok, dst, dst_tok, i, slot):
                    sv = src_ap.rearrange("p (g d) -> p g d", d=64)
                    cosb = ropeT[:, i, 0:64].unsqueeze(1).to_broadcast([128, 16, 64])
                    sinlo = ropeT[:, i, 64:96].unsqueeze(1).to_broadcast([128, 16, 32])
                    sinhi = ropeT[:, i, 96:128].unsqueeze(1).to_broadcast([128, 16, 32])
                    A = rA[slot][:].rearrange("p (g d) -> p g d", d=64)
                    Bv = rB[slot][:].rearrange("p (g d) -> p g d", d=64)
                    S.op(eng, lambda e: e.tensor_tensor(out=A, in0=sv, in1=cosb, op=ALU.mult),
                         reads=[src_tok, tk("ropeT")], writes=[tk("rA", slot)])
                    S.op(eng, lambda e: e.tensor_tensor(out=Bv[:, :, 0:32], in0=sv[:, :, 32:64], in1=sinlo, op=ALU.mult),
                         reads=[src_tok, tk("ropeT")], writes=[tk("rB", slot, 0)])
                    S.op(eng, lambda e: e.tensor_tensor(out=Bv[:, :, 32:64], in0=sv[:, :, 0:32], in1=sinhi, op=ALU.mult),
                         reads=[src_tok, tk("ropeT")], writes=[tk("rB", slot, 1)])
                    S.op(eng, lambda e: e.tensor_tensor(out=dst[:], in0=rA[slot][:], in1=rB[slot][:], op=ALU.add),
                         reads=[tk("rA", slot), tk("rB", slot, 0), tk("rB", slot, 1)], writes=[dst_tok])

                def proA1(i):
                    s = i % 2
                    S.dma("sp", xt[s][:], x_d[i * 128:(i + 1) * 128, :], cx[s], writes=[tk("xt", s)])
                    return prologue_act(PB, tk("xt", s), xt[s][:])

                def proA2(i, ps):
                    s = i % 2
                    tp, tptok = prologue_pe(PB, ps)
                    S.op("dve", lambda e, s=s, tp=tp: e.tensor_tensor(out=xT[s][:], in0=tp[:], in1=gcol_bc(0), op=ALU.mult), reads=[tptok], writes=[tk("xT", s)])

                def mmA(i):
                    s = i % 2
                    for (pp, name, off) in ((pq, "pq", 0), (pk, "pk", D), (pv, "pv", 2 * D)):
                        def mm(e, pp=pp, off=off, s=s):
                            for n in range(2):
                                for c in range(8):
                                    r = e.matmul(pp[:, n * 512:(n + 1) * 512], lhsT=xT[s][:, c, :], rhs=wq[:, c, off + n * 512: off + (n + 1) * 512],
                                                 start=(c == 0), stop=(c == 7))
                            return r
                        S.op("pe", mm, reads=[tk("xT", s), tk("wq", off // 512), tk("wq", off // 512 + 1)], writes=[tk(name)])

                def evacA1(i):
                    s = i % 2; gi = i // 4; gs = gi % 2; sub = i % 4
                    rope("dve", pq[:], tk("pq"), qkb[0][i % 2], tk("qkb", 0, i % 2), i, 0)
                    S.op("act", lambda e: e.activation(out=ksb[:], in_=pk[:], func=AF.Copy), reads=[tk("pk")], writes=[tk("ksb")])
                    rope("pool", ksb[:], tk("ksb"), qkb[1][i % 2], tk("qkb", 1, i % 2), i, 1)
                    S.op("act", lambda e, gs=gs, sub=sub: e.activation(out=V_acc[gs][:, :, sub, 0:128], in_=pv[:].rearrange("p (h e) -> p h e", e=128), func=AF.Copy),
                         reads=[tk("pv")], writes=[tk("Vacc", gs)])

                def evacA2(i):
                    s = i % 2; gi = i // 4; gs = gi % 2; sub = i % 4
                    for (which, acc, accn) in ((0, qT_acc, "qTa"), (1, kT_acc, "kTa")):
                        def trq(e, which=which, i=i):
                            for h in range(8):
                                r = e.transpose(out=tq[:, h, :], in_=qkb[which][i % 2][:, h * 128:(h + 1) * 128], identity=ident[:])
                            return r
                        S.op("pe", trq, reads=[tk("qkb", which, i % 2)], writes=[tk("tq")])
                        ceng = "dve" if which == 0 else "act"
                        if ceng == "dve":
                            S.op("dve", lambda e, acc=acc, gs=gs, sub=sub: e.tensor_copy(out=acc[gs][:, :, sub * 128:(sub + 1) * 128], in_=tq[:]),
                                 reads=[tk("tq")], writes=[tk(accn, gs)])
                        else:
                            S.op("act", lambda e, acc=acc, gs=gs, sub=sub: e.activation(out=acc[gs][:, :, sub * 128:(sub + 1) * 128], in_=tq[:], func=AF.Copy),
                                 reads=[tk("tq")], writes=[tk(accn, gs)])
                    if sub == 3 or i == NTL - 1:
                        nt = (sub + 1) * 128
                        t0 = gi * 512
                        S.dma("pool", QT_s[:, :, t0:t0 + nt].rearrange("h p t -> p h t"), qT_acc[gs][:, :, 0:nt], cst[gs],
                              reads=[tk("qTa", gs)], writes=[tk("QT_s", gi)])
                        S.dma("pool", KT_s[:, :, t0:t0 + nt].rearrange("h p t -> p h t"), kT_acc[gs][:, :, 0:nt], cst[gs],
                              reads=[tk("kTa", gs)], writes=[tk("KT_s", gi)])
                        S.dma("pool", V_s[:, :, gi * 4:gi * 4 + sub + 1, :].rearrange("h p j e -> p h j e"), V_acc[gs][:, :, 0:sub + 1, :], cst[gs],
                              reads=[tk("Vacc", gs)], writes=[tk("V_s", gi)])


                proA2(0, proA1(0)); mmA(0)
                psn = proA1(1) if NTL > 1 else None
                for i in range(NTL):
                    evacA1(i)
                    if i + 1 < NTL:
                        proA2(i + 1, psn)
                        mmA(i + 1)
                    if i + 2 < NTL:
                        psn = proA1(i + 2)
                    evacA2(i)
                S.flush()

        sc1 = ExitStack(); sc2 = ExitStack()
        W1b = sbt(sc1, "W1b", [128, 8, DFF], BF16); Wzb = sbt(sc1, "Wzb", [128, 8, D], BF16)
        W2box = [sbt(sc2, "W2b", [128, 32, D], BF16)]
        cwm = [S.chan("M_w1"), S.chan("M_w2"), S.chan("M_wz")]

        def load_w1z(w1d, wz_d):
            for c in range(8):
                S.dma("pool", Wzb[:, c, :], wz_d[c * 128:(c + 1) * 128, :], cwm[2], writes=[tk("Wz", c)])
            for c in range(8):
                S.dma("pool", W1b[:, c, :], w1d[c * 128:(c + 1) * 128, :], cwm[0], writes=[tk("W1", c)])

        def load_w2(w2d):
            for c4 in range(8):
                S.dma("pool", W2box[0][:, c4 * 4:(c4 + 1) * 4, :], w2d[c4 * 512:(c4 + 1) * 512, :].rearrange("(c p) d -> p c d", p=128), cwm[1], writes=[tk("W2", c4)])

        if "B" in phases:
            with ExitStack() as st:
                QB = 512
                NQ = NT // QB
                NG = (NT + 511) // 512
                K0p = sbt(st, "K0p", [128, NT], BF16)
                K1p = sbt(st, "K1p", [128, NT], BF16)
                QTq = [sbt(st, f"QTq{i}", [128, QB], BF16) for i in range(2)]
                V_h = sbt(st, "Vh", [128, NTL, 129], BF16)
                NPT = 3
                PT = [sbt(st, f"PT{i}", [128, 2, QB], BF16) for i in range(NPT)]
                lamv = sbt(st, "lamv", [128, 256], F32); g8col = sbt(st, "g8col", [128, 1], F32)
                ltmp = sbt(st, "ltmp", [128, 128], F32); lsum = sbt(st, "lsum", [128, 2], F32)
                nlam = sbt(st, "nlam", [128, 1], F32)
                ones_b = sbt(st, "ones_b", [128, 128], BF16)
                ones_f = sbt(st, "ones_f", [128, 128], F32)
                Oc = [sbt(st, f"Oc{i}", [128, 2, QB], F32) for i in range(2)]
                bcs = [sbt(st, f"bcs{i}", [128, 2, QB], F32) for i in range(2)]
                sq = [sbt(st, f"sq{i}", [128, QB], BF16) for i in range(2)]
                rr = [sbt(st, f"rr{i}", [128, QB], F32) for i in range(2)]
                OTo = [sbt(st, f"OTo{i}", [128, QB], BF16) for i in range(2)]
                Sps = pst(st, "Sps", [128, 2, 2, 512], F32)
                OTp = pst(st, "OTp", [128, 2, 512], F32)
                denp = pst(st, "denp", [128, 2, 512], F32)
                ck = [S.chan(f"B_k{g}") for g in range(NG)]; cv = [S.chan(f"B_v{g}") for g in range(NG)]; cq = [S.chan("B_q0"), S.chan("B_q1")]
                cl = S.chan("B_l"); cs2 = S.chan("B_s"); co = [S.chan("B_o0"), S.chan("B_o1")]; cb = [S.chan("B_b0"), S.chan("B_b1")]
                if "C" in phases:
                    load_w1z(w1_d[0], woa_d); load_w2(w2_d[0])
                S.dma("sp", lamv[:], lamv_d.partition_broadcast(128), cl, writes=[tk("lamv")])
                with nc.allow_non_contiguous_dma(reason="tiny 128-element column load"):
                    S.dma("sp", g8col[:], subg_d.rearrange("o e -> e o"), cs2, writes=[tk("g8col")])
                for i in range(2):
                    S.op("dve", lambda e, i=i: e.tensor_tensor(out=ltmp[:, i * 64:(i + 1) * 64], in0=lamv[:, i * 128:i * 128 + 64],
                                                                in1=lamv[:, i * 128 + 64:i * 128 + 128], op=ALU.mult),
                         reads=[tk("lamv")], writes=[tk("ltmp")])
                    S.op("dve", lambda e, i=i: e.reduce_sum(out=lsum[:, i:i + 1], in_=ltmp[:, i * 64:(i + 1) * 64], axis=AX.X),
                         reads=[tk("ltmp")], writes=[tk("lsum")])
                S.op("act", lambda e: e.activation(out=lsum[:], in_=lsum[:], func=AF.Exp), reads=[tk("lsum")], writes=[tk("lsum")])
                S.op("dve", lambda e: e.tensor_tensor(out=nlam[:], in0=lsum[:, 1:2], in1=lsum[:, 0:1], op=ALU.subtract),
                     reads=[tk("lsum")], writes=[tk("nlam")])
                S.op("dve", lambda e: e.tensor_scalar(out=nlam[:], in0=nlam[:], scalar1=-LAMBDA_INIT, scalar2=None, op0=ALU.add),
                     reads=[tk("nlam")], writes=[tk("nlam")])
                S.op("dve", lambda e: e.tensor_scalar(out=g8col[:], in0=g8col[:], scalar1=1.0 - LAMBDA_INIT, scalar2=None, op0=ALU.mult),
                     reads=[tk("g8col")], writes=[tk("g8col")])
                S.op("dve", lambda e: e.memset(ones_b[:], 1.0), writes=[tk("ones_b")])
                S.op("dve", lambda e: e.memset(ones_f[:], 1.0), writes=[tk("ones_f")])
                S.op("pool", lambda e: e.memset(K0p[64:128, :], 0.0), writes=[tk("Kz")])
                S.op("pool", lambda e: e.memset(K1p[0:64, :], 0.0), writes=[tk("Kz")])

                def load_kv(hh, g):
                    t0 = g * 512; t1 = min(NT, t0 + 512)
                    S.dma("sp", K0p[0:64, t0:t1], KT_s[hh, 0:64, t0:t1], ck[g], writes=[tk("K", g)])
                    S.dma("sp", K1p[64:128, t0:t1], KT_s[hh, 64:128, t0:t1], ck[g], writes=[tk("K", g)])
                    S.dma("sp", V_h[:, t0 // 128:t1 // 128, :], V_s[hh, :, t0 // 128:t1 // 128, :], cv[g], writes=[tk("V", g)])
                SI = [0]
                pidx = 0
                bidx = 0
                pending = []
                GJ = [0]
                t2b = {}

                def release(boundary):
                    held = set()
                    for ent in list(pending):
                        bid_ = ent["bid"]
                        if bid_ in held:
                            continue
                        ok = ent["rel"] <= GJ[0]
                        if ent["k"] == 3 and boundary and bid_ in t2b and GJ[0] >= t2b[bid_] + 2:
                            ok = True
                        if not ok:
                            held.add(bid_)
                            continue
                        pending.remove(ent)
                        ent["fn"]()
                        if ent["k"] == 2:
                            t2b[bid_] = GJ[0]
                        if ent["k"] == 3:
                            for e2 in pending:
                                if e2["bid"] == bid_ and e2["k"] == 4:
                                    e2["rel"] = GJ[0] + 2

                def make_epilogue(h, Q, eb):
                    t0 = Q * QB
                    def s0():
                        for c in range(2):
                            S.op("dve", lambda e, c=c: e.tensor_copy(out=Oc[eb][:, c, :], in_=OTp[:, c, :]), reads=[tk("OTp", c)], writes=[tk("Oc", eb)])
                        for c in range(2):
                            S.op("dve", lambda e, c=c: e.tensor_copy(out=bcs[eb][:, c, :], in_=denp[:, c, :]), reads=[tk("denp", c)], writes=[tk("bcs", eb)])
                    def s1():
                        for c in range(2):
                            S.op("dve", lambda e, c=c: e.reciprocal(out=bcs[eb][:, c, :], in_=bcs[eb][:, c, :]), reads=[tk("bcs", eb)], writes=[tk("bcs", eb)])
                        S.op("dve", lambda e: e.tensor_scalar(out=bcs[eb][:, 1, :], in0=bcs[eb][:, 1, :], scalar1=nlam[:], scalar2=None, op0=ALU.mult),
                             reads=[tk("bcs", eb), tk("nlam")], writes=[tk("bcs", eb)])
                    def s2():
                        S.op("dve", lambda e: e.tensor_tensor(out=Oc[eb][:], in0=Oc[eb][:], in1=bcs[eb][:], op=ALU.mult),
                             reads=[tk("Oc", eb), tk("bcs", eb)], writes=[tk("Oc", eb)])
                        S.op("dve", lambda e: e.tensor_tensor(out=Oc[eb][:, 0, :], in0=Oc[eb][:, 0, :], in1=Oc[eb][:, 1, :], op=ALU.add),
                             reads=[tk("Oc", eb)], writes=[tk("Oc", eb)])
                    def s2b():
                        S.op("act", lambda e: e.activation(out=sq[eb][:], in_=Oc[eb][:, 0, :], func=AF.Square), reads=[tk("Oc", eb)], writes=[tk("sq", eb)])
                    def s3():
                        sl = SI[0] % 2
                        S.op("pe", lambda e: e.matmul(Sps[:, 0, sl, :], lhsT=ones_b[:], rhs=sq[eb][:], start=True, stop=True), reads=[tk("sq", eb), tk("ones_b")], writes=[tk("Sps", sl)])
                        S.op("act", lambda e: e.activation(out=rr[eb][:], in_=Sps[:, 0, sl, :], func=AF.Ln, scale=1.0 / 128.0, bias=eps5[:]), reads=[tk("Sps", sl)], writes=[tk("rr", eb)])
                        S.op("act", lambda e: e.activation(out=rr[eb][:], in_=rr[eb][:], func=AF.Exp, scale=-0.5), reads=[tk("rr", eb)], writes=[tk("rr", eb)])
                    def s4():
                        S.op("dve", lambda e: e.scalar_tensor_tensor(out=OTo[eb][:], in0=Oc[eb][:, 0, :], scalar=g8col[:], in1=rr[eb][:], op0=ALU.mult, op1=ALU.mult),
                             reads=[tk("Oc", eb), tk("rr", eb), tk("g8col")], writes=[tk("OTo", eb)])
                        S.dma("pool", OT_s[h, :, t0:t0 + QB], OTo[eb][:], co[eb], reads=[tk("OTo", eb)], writes=[tk("OT_s", h)])
                    return [s0, s1, s2, s2b, s3, s4]

                def load_q(bi):
                    hh, QQ = bi // NQ, bi % NQ
                    S.dma("sp", QTq[bi % 2][:], QT_s[hh, :, QQ * QB:(QQ + 1) * QB], cq[bi % 2], writes=[tk("QTq", bi % 2)])
                load_q(0)
                for g in range(NG):
                    load_kv(0, g)
                for h in range(NH):
                    hs = h % 2
                    for Q in range(NQ):
                        nj = 4 * Q + 4
                        q0 = Q * QB
                        eb = bidx % 2
                        qs = bidx % 2
                        if bidx + 1 < NH * NQ:
                            load_q(bidx + 1)
                        bidx += 1

                        def emit_S(j, sl):
                            cs = 128 * max(0, j - 4 * Q)

                            def f(e, j=j, sl=sl, qs=qs, cs=cs):
                                e.matmul(Sps[:, 0, sl, cs:QB], lhsT=K0p[:, j * 128:(j + 1) * 128], rhs=QTq[qs][:, cs:QB], start=True, stop=True)
                                return e.matmul(Sps[:, 1, sl, cs:QB], lhsT=K1p[:, j * 128:(j + 1) * 128], rhs=QTq[qs][:, cs:QB], start=True, stop=True)
                            S.op("pe", f, reads=[tk("K", j // 4), tk("Kz"), tk("QTq", qs)], writes=[tk("Sps", sl)])
                        slots = {}
                        slots[0] = SI[0] % 2; SI[0] += 1
                        emit_S(0, slots[0])
                        for j in range(nj):
                            if j + 1 < nj:
                                slots[j + 1] = SI[0] % 2; SI[0] += 1
                                emit_S(j + 1, slots[j + 1])
                            sl = slots[j]
                            ps_ = pidx % NPT; pidx += 1
                            jj = j - 4 * Q
                            c0 = 128 * jj if jj > 0 else 0
                            S.op("act", lambda e, sl=sl, ps_=ps_, c0=c0: e.activation(out=PT[ps_][:, :, c0:QB], in_=Sps[:, :, sl, c0:QB], func=AF.Exp, scale=0.125),
                                 reads=[tk("Sps", sl)], writes=[tk("PT", ps_)])
                            if jj >= 0:
                                S.op("pool", lambda e, ps_=ps_, c0=c0: e.memset(PT[ps_][64:128, :, c0:c0 + 64], 0.0), reads=[], writes=[tk("PT", ps_)])

                            for c in range(2):
                                S.op("pe", lambda e, j=j, ps_=ps_, c0=c0, nj=nj, c=c: e.matmul(OTp[:, c, c0:QB], lhsT=V_h[:, j, 0:128], rhs=PT[ps_][:, c, c0:QB],
                                                                                                  start=(j == 0), stop=(j == nj - 1), skip_group_check=True),
                                     reads=[tk("PT", ps_), tk("V", j // 4)], writes=[tk("OTp", c)])
                            for c in range(2):
                                S.op("pe", lambda e, j=j, ps_=ps_, c0=c0, nj=nj, c=c: e.matmul(denp[:, c, c0:QB], lhsT=ones_b[:], rhs=PT[ps_][:, c, c0:QB],
                                                                                                  start=(j == 0), stop=(j == nj - 1), skip_group_check=True),
                                     reads=[tk("PT", ps_), tk("ones_b")], writes=[tk("denp", c)])
                            if Q == NQ - 1 and j % 4 == 3 and h + 1 < NH:
                                load_kv(h + 1, j // 4)
                            GJ[0] += 1
                            release(False)
                        release(True)
                        for ent in [x for x in pending if x["eb"] == eb]:
                            pending.remove(ent); ent["fn"]()
                        ep = make_epilogue(h, Q, eb)
                        ep[0]()
                        bid = bidx
                        for k, (off, fn) in enumerate(zip((1, 6, 9, 40, 42), ep[1:])):
                            pending.append({"rel": GJ[0] + off, "eb": eb, "fn": fn, "k": k, "bid": bid})
                while pending:
                    pending.pop(0)["fn"]()
                S.flush()

        def mlp_phase(tag, xin_d, ZTs, g_row, xout_d, final, W2b, pre_hook=None):
            with ExitStack() as st:
                NB = NT // 256
                gfb = sbt(st, "gfb", [128, D], F32) if final else None
                ss = [sbt(st, f"ss{i}", [128, 1], F32) for i in range(2)]
                rstd = [sbt(st, f"rstd{i}", [128, 1], F32) for i in range(2)]
                xb = [sbt(st, f"xb{i}", [128, D], BF16) for i in range(2)]
                xn1 = sbt(st, "xn", [128, D], F32) if final else None; xn = [xn1, xn1]
                tp = [pst(st, f"tp{i}", [128, 8, 128], BF16) for i in range(2)]
                xt = [sbt(st, f"xt{i}", [128, 2, D], F32) for i in range(2)]
                zT1 = sbt(st, "zT", [128, 8, 256], BF16); zT = [zT1, zT1]
                xT = [sbt(st, f"xT{i}", [128, 8, 256], BF16) for i in range(2)]
                hidT = sbt(st, "hidT", [128, 32, 256], BF16)
                NR = 2
                rt = [sbt(st, f"rt{i}", [128, 256], F32) for i in range(NR)]
                ss2 = sbt(st, "ss2", [128, 1], F32); rs2 = sbt(st, "rs2", [128, 1], F32)
                yps = [pst(st, f"yps{i}", [128, D], F32) for i in range(2)]
                zps = [pst(st, f"zps{i}", [128, 512], F32) for i in range(2)]
                cg = S.chan(tag + "_g"); cgf = S.chan(tag + "_gf")
                cx = [S.chan(tag + "_x0"), S.chan(tag + "_x1")]; cz = [S.chan(tag + "_z0"), S.chan(tag + "_z1")]
                cso = [S.chan(tag + "_o0"), S.chan(tag + "_o1")]
                if pre_hook is not None:
                    pre_hook()
                if final:
                    S.dma("sp", gfb[:], gvec_d[4:5, :].partition_broadcast(128), cgf, writes=[tk("gfb")])
                W1r = [tk("W1", c) for c in range(8)]; Wzr = [tk("Wz", c) for c in range(8)]; W2r = [tk("W2", c) for c in range(8)]
                cnt = {"yi": 0, "zi": 0, "ri": 0}

                def load_x(b):
                    s = b % 2; t0 = b * 256
                    S.dma("sp", xt[s][:], xin_d[t0:t0 + 256, :].rearrange("(t p) d -> p t d", p=128), cx[s], writes=[tk("xt", s, 0), tk("xt", s, 1)])

                def load_z(b):
                    s = b % 2; t0 = b * 256
                    S.dma("sp", zT[s][:], ZTs[:, :, t0:t0 + 256].rearrange("c p t -> p c t"), cz[0], writes=[tk("zT")])

                def pre_y0(b):
                    s = b % 2
                    for t in range(2):
                        ys = cnt["yi"] % 2; cnt["yi"] += 1

                        def mm0(e, s=s, t=t, ys=ys):
                            for n in range(2):
                                for c in range(8):
                                    r = e.matmul(yps[ys][:, n * 512:(n + 1) * 512], lhsT=zT[s][:, c, t * 128:(t + 1) * 128], rhs=Wzb[:, c, n * 512:(n + 1) * 512],
                                                 start=(c == 0), stop=(c == 7))
                            return r
                        S.op("pe", mm0, reads=[tk("zT")] + Wzr, writes=[tk("yps", ys)])
                        S.op("dve", lambda e, s=s, t=t, ys=ys: e.tensor_tensor(out=xt[s][:, t, :], in0=xt[s][:, t, :], in1=yps[ys][:], op=ALU.add),
                             reads=[tk("yps", ys), tk("xt", s, t)], writes=[tk("xt", s, t)])

                def pre_norm(b):
                    s = b % 2
                    for t in range(2):
                        S.op("act", lambda e, s=s, t=t: e.activation(out=xb[t][:], in_=xt[s][:, t, :], func=AF.Square, accum_out=ss[t][:]),
                             reads=[tk("xt", s, t)], writes=[tk("xb", t), tk("ss", t)])
                        rstd_ops(ss[t][:], rstd[t][:], float(D), eps6, [tk("ss", t)], [tk("rstd", t)])
                        S.op("act", lambda e, s=s, t=t: e.activation(out=xb[t][:], in_=xt[s][:, t, :], func=AF.Copy, scale=rstd[t][:]),
                             reads=[tk("xt", s, t), tk("rstd", t)], writes=[tk("xb", t)])

                def pre_mult(b):
                    pass

                def pre_tr(b):
                    for t in range(2):
                        def tr(e, t=t):
                            for c in range(8):
                                i = e.transpose(out=tp[t][:, c, :], in_=xb[t][:, c * 128:(c + 1) * 128], identity=ident[:])
                            return i
                        S.op("pe", tr, reads=[tk("xb", t)], writes=[tk("tp", t)])

                def pre_copy(b):
                    for t in range(2):
                        S.op("dve", lambda e, t=t, b=b: e.tensor_tensor(out=xT[b % 2][:, :, t * 128:(t + 1) * 128], in0=tp[t][:], in1=gcol_bc(g_row), op=ALU.mult),
                             reads=[tk("tp", t)], writes=[tk("xT", b % 2)])

                def mm1_range(b, f0, f1):
                    for f in range(f0, f1):
                        zs = cnt["zi"] % 2; cnt["zi"] += 1
                        zp = zps[zs][:, 0:256]

                        def mm1(e, f=f, zp=zp, b=b):
                            for c in range(8):
                                r = e.matmul(zp, lhsT=W1b[:, c, f * 128:(f + 1) * 128], rhs=xT[b % 2][:, c, :], start=(c == 0), stop=(c == 7))
                            return r
                        S.op("pe", mm1, reads=[tk("xT", b % 2)] + W1r, writes=[tk("zps", zs)])
                        r_ = cnt["ri"] % NR; cnt["ri"] += 1
                        S.op("act", lambda e, zp=zp, r_=r_: e.activation(out=rt[r_][:], in_=zp, func=AF.Relu), reads=[tk("zps", zs)], writes=[tk("rt", r_)])
                        S.op("pool", lambda e, f=f, r_=r_: e.tensor_tensor(out=hidT[:, f, :], in0=rt[r_][:], in1=rt[r_][:], op=ALU.mult),
                             reads=[tk("rt", r_)], writes=[tk("hidT", f)])

                def mm2_t(b, t):
                    s = b % 2
                    ys = cnt["yi"] % 2; cnt["yi"] += 1
                    for n in range(2):
                        for fg in range(4):
                            def mm2(e, t=t, ys=ys, n=n, fg=fg):
                                for f in range(fg * 8, fg * 8 + 8):
                                    r = e.matmul(yps[ys][:, n * 512:(n + 1) * 512], lhsT=hidT[:, f, t * 128:(t + 1) * 128], rhs=W2b[:, f, n * 512:(n + 1) * 512],
                                                 start=(f == 0), stop=(f == 31))
                                return r
                            S.op("pe", mm2, reads=[tk("hidT", f) for f in range(fg * 8, fg * 8 + 8)] + W2r, writes=[tk("yps", ys)])
                    S.op("dve", lambda e, s=s, t=t, ys=ys: e.tensor_tensor(out=xt[s][:, t, :], in0=xt[s][:, t, :], in1=yps[ys][:], op=ALU.add),
                         reads=[tk("yps", ys), tk("xt", s, t)], writes=[tk("xt", s, t)])
                    if final:
                        S.op("act", lambda e, s=s, t=t: e.activation(out=xn[t][:], in_=xt[s][:, t, :], func=AF.Square, accum_out=ss2[:]),
                             reads=[tk("xt", s, t)], writes=[tk("xn"), tk("ss2")])
                        rstd_ops(ss2[:], rs2[:], float(D), eps6, [tk("ss2")], [tk("rs2")])
                        S.op("dve", lambda e, s=s, t=t: e.scalar_tensor_tensor(out=xt[s][:, t, :], in0=xt[s][:, t, :], scalar=rs2[:], in1=gfb[:], op0=ALU.mult, op1=ALU.mult),
                             reads=[tk("xt", s, t), tk("rs2"), tk("gfb")], writes=[tk("xt", s, t)])

                def store(b):
                    s = b % 2; t0 = b * 256
                    S.dma("sp", xout_d[t0:t0 + 256, :].rearrange("(t p) d -> p t d", p=128), xt[s][:], cso[s], reads=[tk("xt", s, 0), tk("xt", s, 1)], writes=[tk("xout", b)])

                load_x(0); load_z(0)
                if NB > 1:
                    load_x(1)
                pre_y0(0)
                if NB > 1:
                    load_z(1)
                pre_norm(0); pre_mult(0); pre_tr(0); pre_copy(0)
                for b in range(NB):
                    nxt = b + 1 < NB
                    mm1_range(b, 0, 8)
                    if nxt:
                        pre_y0(b + 1)
                        if b + 2 < NB:
                            load_z(b + 2)
                    mm1_range(b, 8, 16)
                    if nxt:
                        pre_norm(b + 1)
                    mm1_range(b, 16, 24)
                    if nxt:
                        pre_mult(b + 1)
                    mm1_range(b, 24, 32)
                    if nxt:
                        pre_tr(b + 1)
                    mm2_t(b, 0)
                    if nxt:
                        pre_copy(b + 1)
                    mm2_t(b, 1)
                    store(b)
                    if b + 2 < NB:
                        load_x(b + 2)
                S.flush()

        if "C" in phases:
            hook = None if "B" in phases else (lambda: (load_w1z(w1_d[0], woa_d), load_w2(w2_d[0])))
            mlp_phase("C", x_d, OT_s, 1, x2_s, False, W2box[0], pre_hook=hook)
        sc2.close()

        if "D" in phases:
            with ExitStack() as st:
                TB = 512 if NT >= 512 else NT
                NBK = NT // TB
                NTB = TB // 128
                Wxb = sbt(st, "Wxb", [128, 8, D], BF16); Wyb = sbt(st, "Wyb", [128, 8, D], BF16)
                Wab = sbt(st, "Wab", [128, 4, 2, 256], BF16); Wib = sbt(st, "Wib", [128, 4, 2, 256], BF16)
                gbc = None
                PB = mk_prologue_bufs(st, 1)
                xt = [sbt(st, f"xt{i}", [128, D], F32) for i in range(2)]
                xT = [sbt(st, f"xT{i}", [128, 8, TB], BF16) for i in range(2)]
                xpre = [sbt(st, f"xpre{i}", [128, TB + 3], F32) for i in range(4)]
                halo = sbt(st, "halo", [128, 8, 4], F32); one1 = sbt(st, "one1", [128, 1], F32)
                xc = [sbt(st, f"xc{i}", [128, TB], F32) for i in range(4)]
                xcb = [sbt(st, f"xcb{i}", [128, TB], BF16) for i in range(4)]
                gate = [sbt(st, f"gate{i}", [128, TB], BF16) for i in range(4)]
                g1 = [sbt(st, f"g1{i}", [128, TB], F32) for i in range(2)]
                ga = [sbt(st, f"ga{i}", [128, TB], F32) for i in range(2)]; gi_ = [sbt(st, f"gi{i}", [128, TB], F32) for i in range(2)]
                aa = [sbt(st, f"aa{i}", [128, TB], F32) for i in range(2)]; m2 = [sbt(st, f"m2{i}", [128, TB], F32) for i in range(2)]
                hs_ = [sbt(st, f"hs{i}", [128, TB], F32) for i in range(2)]
                yT_acc = [sbt(st, "yTa0", [128, 8, TB], BF16)]
                hlast = sbt(st, "hlast", [128, 8], F32)
                nb = sbt(st, "nb", [128, 16], F32)
                sp8 = sbt(st, "sp8", [128, 8], F32); spe = sbt(st, "spe", [128, 8], F32); spt = sbt(st, "spt", [128, 8], F32)
                psx1 = pst(st, "psx0", [128, 512], F32); psx = [psx1, psx1]
                psy = [pst(st, f"psy{i}", [128, 512], F32) for i in range(2)]
                psa = [pst(st, f"psa{i}", [128, 512], F32) for i in range(2)]; psi = [pst(st, f"psi{i}", [128, 512], F32) for i in range(2)]
                cw = [S.chan("D_wx"), S.chan("D_wy"), S.chan("D_wa"), S.chan("D_wi")]
                cg = S.chan("D_g"); cx = [S.chan("D_x0"), S.chan("D_x1")]; cso = [S.chan("D_o0"), S.chan("D_o1")]
                for n in range(4):
                    S.dma("pool", Wxb[:, :, n * 256:(n + 1) * 256], wx_d[:, n * 256:(n + 1) * 256].rearrange("(c p) f -> p c f", p=128), cw[n], writes=[tk("Wx", n)])
                    S.dma("pool", Wyb[:, :, n * 256:(n + 1) * 256], wy_d[:, n * 256:(n + 1) * 256].rearrange("(c p) f -> p c f", p=128), cw[n], writes=[tk("Wy", n)])
                    S.dma("pool", Wab[:, n, :, :], wa_d[n].rearrange("(i p) o -> p i o", p=128), cw[n], writes=[tk("Wa", n)])
                    S.dma("pool", Wib[:, n, :, :], wi_d[n].rearrange("(i p) o -> p i o", p=128), cw[n], writes=[tk("Wi", n)])
                if "E" in phases:
                    load_w1z(w1_d[1], wor_d)
                S.op("dve", lambda e: e.tensor_scalar(out=nb[:], in0=cvec[:, 40:56], scalar1=-1.0, scalar2=None, op0=ALU.mult), writes=[tk("nb")])
                S.op("act", lambda e: e.activation(out=spe[:], in_=cvec[:, 56:64], func=AF.Exp, scale=-1.0), writes=[tk("spe")])
                S.op("dve", lambda e: e.tensor_scalar(out=spt[:], in0=spe[:], scalar1=1.0 / 3.0, scalar2=-0.5, op0=ALU.mult, op1=ALU.add), reads=[tk("spe")], writes=[tk("spt")])
                S.op("dve", lambda e: e.tensor_tensor(out=spt[:], in0=spt[:], in1=spe[:], op=ALU.mult), reads=[tk("spt"), tk("spe")], writes=[tk("spt")])
                S.op("dve", lambda e: e.tensor_scalar(out=spt[:], in0=spt[:], scalar1=1.0, scalar2=None, op0=ALU.add), reads=[tk("spt")], writes=[tk("spt")])
                S.op("dve", lambda e: e.tensor_tensor(out=spt[:], in0=spt[:], in1=spe[:], op=ALU.mult), reads=[tk("spt"), tk("spe")], writes=[tk("spt")])
                S.op("dve", lambda e: e.tensor_scalar(out=sp8[:], in0=spt[:], scalar1=-8.0, scalar2=None, op0=ALU.mult), reads=[tk("spt")], writes=[tk("sp8")])
                S.op("dve", lambda e: e.memset(hlast[:], 0.0), writes=[tk("hlast")])
                S.op("pool", lambda e: e.memset(halo[:], 0.0), writes=[tk("halo", c) for c in range(8)])
                S.op("pool", lambda e: e.memset(one1[:], 1.0), writes=[tk("one1")])
                XI = [0]

                def pro(blk):
                    for t in range(NTB):
                        s = XI[0] % 2; XI[0] += 1
                        i = blk * NTB + t
                        S.dma("sp", xt[s][:], x2_s[i * 128:(i + 1) * 128, :], cx[s], reads=[tk("x2")], writes=[tk("xt", s)])
                        tp, tptok = prologue(PB, tk("xt", s), xt[s][:], None, gbc)
                        S.op("dve", lambda e, t=t, tp=tp, blk=blk: e.tensor_tensor(out=xT[blk % 2][:, :, t * 128:(t + 1) * 128], in0=tp[:], in1=gcol_bc(2), op=ALU.mult),
                             reads=[tptok], writes=[tk("xT", blk % 2)])

                def stA(g):
                    blk, n = g // 4, g % 4
                    pair = (2 * n, 2 * n + 1)
                    for cc in pair:
                        xs = cc % 2
                        for (ps_, W_, nm, xk) in ((psx[xs], Wxb, "psx", 0), (psy[xs], Wyb, "psy", xs)):
                            def mmxy(e, ps_=ps_, W_=W_, cc=cc, blk=blk):
                                for c in range(8):
                                    r = e.matmul(ps_[:, 0:TB], lhsT=W_[:, c, cc * 128:(cc + 1) * 128], rhs=xT[blk % 2][:, c, :], start=(c == 0), stop=(c == 7))
                                return r
                            S.op("pe", mmxy, reads=[tk("xT", blk % 2), tk("Wx", cc // 2), tk("Wy", cc // 2)], writes=[tk(nm, xk)])
                        q4 = cc % 4
                        S.op("dve", lambda e, cc=cc, q4=q4: e.tensor_copy(out=xpre[q4][:, 0:3], in_=halo[:, cc, 0:3]), reads=[tk("halo", cc)], writes=[tk("xpre", q4)])
                        S.op("act", lambda e, q4=q4, xs=xs: e.activation(out=xpre[q4][:, 3:TB + 3], in_=psx[xs][:, 0:TB], func=AF.Copy),
                             reads=[tk("psx", 0)], writes=[tk("xpre", q4)])
                        S.op("dve", lambda e, cc=cc, q4=q4: e.tensor_copy(out=halo[:, cc, 0:3], in_=xpre[q4][:, TB:TB + 3]), reads=[tk("xpre", q4)], writes=[tk("halo", cc)])

                def stB(g):
                    blk, n = g // 4, g % 4
                    pair = (2 * n, 2 * n + 1)
                    for cc in pair:
                        xs = cc % 2; q4 = cc % 4
                        S.op("act", lambda e, xs=xs: e.activation(out=g1[xs][:], in_=psy[xs][:, 0:TB], func=AF.Square), reads=[tk("psy", xs)], writes=[tk("g1", xs)])

                def stC(g):
                    blk, n = g // 4, g % 4
                    pair = (2 * n, 2 * n + 1)
                    for cc in pair:
                        q4 = cc % 4
                        S.op("pool", lambda e, cc=cc, q4=q4: e.tensor_scalar(out=xc[q4][:], in0=xpre[q4][:, 3:TB + 3], scalar1=cvec[:, 24 + cc:25 + cc], scalar2=cvec[:, 32 + cc:33 + cc],
                                                                              op0=ALU.mult, op1=ALU.add), reads=[tk("xpre", q4)], writes=[tk("xc", q4)])
                        for j in (2, 1, 0):
                            S.op("dve", lambda e, cc=cc, q4=q4, j=j: e.scalar_tensor_tensor(out=xc[q4][:], in0=xpre[q4][:, j:TB + j], scalar=cvec[:, j * 8 + cc:j * 8 + cc + 1], in1=xc[q4][:],
                                                                                             op0=ALU.mult, op1=ALU.add), reads=[tk("xpre", q4), tk("xc", q4)], writes=[tk("xc", q4)])
                        S.op("act", lambda e, q4=q4: e.activation(out=xcb[q4][:], in_=xc[q4][:], func=AF.Copy), reads=[tk("xc", q4)], writes=[tk("xcb", q4)])

                def stD(g):
                    blk, n = g // 4, g % 4
                    pair = (2 * n, 2 * n + 1)
                    for cc in pair:
                        xs = cc % 2
                        S.op("dve", lambda e, xs=xs: e.tensor_scalar(out=g1[xs][:], in0=g1[xs][:], scalar1=GC, scalar2=1.0, op0=ALU.mult, op1=ALU.add),
                             reads=[tk("g1", xs)], writes=[tk("g1", xs)])
                        S.op("dve", lambda e, xs=xs: e.tensor_tensor(out=g1[xs][:], in0=g1[xs][:], in1=psy[xs][:, 0:TB], op=ALU.mult),
                             reads=[tk("g1", xs), tk("psy", xs)], writes=[tk("g1", xs)])
                        S.op("act", lambda e, xs=xs: e.activation(out=g1[xs][:], in_=g1[xs][:], func=AF.Sigmoid, scale=GK), reads=[tk("g1", xs)], writes=[tk("g1", xs)])
                        S.op("dve", lambda e, xs=xs, cc=cc: e.tensor_tensor(out=gate[cc % 4][:], in0=g1[xs][:], in1=psy[xs][:, 0:TB], op=ALU.mult),
                             reads=[tk("g1", xs), tk("psy", xs)], writes=[tk("gate", cc % 4)])

                def stE(g):
                    blk, n = g // 4, g % 4
                    pair = (2 * n, 2 * n + 1)
                    for oc in pair:
                        ol = oc % 2
                        for (pg, Wg, nm) in ((psa[ol], Wab, "psa"), (psi[ol], Wib, "psi")):
                            def mmg(e, pg=pg, Wg=Wg, n=n, ol=ol):
                                for l in range(2):
                                    r = e.matmul(pg[:, 0:TB], lhsT=Wg[:, n, l, ol * 128:(ol + 1) * 128], rhs=xcb[(2 * n + l) % 4][:], start=(l == 0), stop=(l == 1))
                                return r
                            S.op("pe", mmg, reads=[tk("xcb", (2 * n) % 4), tk("xcb", (2 * n + 1) % 4), tk("Wa", n), tk("Wi", n)], writes=[tk(nm, ol)])
                        S.op("act", lambda e, oc=oc, ol=ol: e.activation(out=ga[ol][:], in_=psa[ol][:, 0:TB], func=AF.Sigmoid, bias=cvec[:, 40 + oc:41 + oc]),
                             reads=[tk("psa", ol)], writes=[tk("ga", ol)])
                        S.op("act", lambda e, oc=oc, ol=ol: e.activation(out=gi_[ol][:], in_=psi[ol][:, 0:TB], func=AF.Sigmoid, bias=cvec[:, 48 + oc:49 + oc]),
                             reads=[tk("psi", ol)], writes=[tk("gi", ol)])
                        S.op("pool", lambda e, ol=ol, oc=oc: e.tensor_tensor(out=gi_[ol][:], in0=gi_[ol][:], in1=xc[oc % 4][:], op=ALU.mult), reads=[tk("gi", ol), tk("xc", oc % 4)], writes=[tk("gi", ol)])

                def stFG(g):
                    blk, n = g // 4, g % 4
                    pair = (2 * n, 2 * n + 1)
                    for oc in pair:
                        ol = oc % 2
                        S.op("act", lambda e, oc=oc, ol=ol: e.activation(out=aa[ol][:], in_=ga[ol][:], func=AF.Exp, scale=sp8[:, oc:oc + 1]), reads=[tk("ga", ol), tk("sp8")], writes=[tk("aa", ol)])
                    for oc in pair:
                        ol = oc % 2
                        S.op("act", lambda e, ol=ol: e.activation(out=m2[ol][:], in_=aa[ol][:], func=AF.Square), reads=[tk("aa", ol)], writes=[tk("m2", ol)])
                    for oc in pair:
                        ol = oc % 2
                        S.op("act", lambda e, ol=ol: e.activation(out=m2[ol][:], in_=m2[ol][:], func=AF.Ln, scale=-1.0, bias=one1[:]), reads=[tk("m2", ol)], writes=[tk("m2", ol)])
                    for oc in pair:
                        ol = oc % 2
                        S.op("act", lambda e, ol=ol: e.activation(out=m2[ol][:], in_=m2[ol][:], func=AF.Exp, scale=0.5), reads=[tk("m2", ol)], writes=[tk("m2", ol)])

                def stH(g):
                    blk, n = g // 4, g % 4
                    pair = (2 * n, 2 * n + 1)
                    for oc in pair:
                        ol = oc % 2; o4 = oc % 4
                        S.op("dve", lambda e, ol=ol: e.tensor_tensor(out=gi_[ol][:], in0=gi_[ol][:], in1=m2[ol][:], op=ALU.mult), reads=[tk("gi", ol), tk("m2", ol)], writes=[tk("gi", ol)])
                        S.op("dve", lambda e, oc=oc, ol=ol: e.tensor_tensor_scan(out=hs_[ol][:], data0=aa[ol][:], data1=gi_[ol][:], initial=hlast[:, oc:oc + 1], op0=ALU.mult, op1=ALU.add),
                             reads=[tk("aa", ol), tk("gi", ol), tk("hlast")], writes=[tk("hs", ol)])
                        S.op("dve", lambda e, oc=oc, ol=ol: e.tensor_copy(out=hlast[:, oc:oc + 1], in_=hs_[ol][:, TB - 1:TB]), reads=[tk("hs", ol)], writes=[tk("hlast")])
                        S.op("pool", lambda e, oc=oc, ol=ol, o4=o4: e.tensor_tensor(out=yT_acc[0][:, oc, :], in0=hs_[ol][:], in1=gate[o4][:], op=ALU.mult),
                             reads=[tk("hs", ol), tk("gate", o4)], writes=[tk("yTa", 0)])

                G = NBK * 4
                pro(0); stA(0); stB(0); stC(0); stD(0)
                for g in range(G):
                    stE(g)
                    if g % 4 == 1 and g // 4 + 1 < NBK:
                        pro(g // 4 + 1)
                    if g + 1 < G:
                        stA(g + 1); stB(g + 1)
                    stFG(g)
                    if g + 1 < G:
                        stC(g + 1)
                        stD(g + 1)
                    stH(g)
                    if g % 4 == 3:
                        blk = g // 4
                        S.dma("sp", ZT_s[:, :, blk * TB:(blk + 1) * TB].rearrange("c p t -> p c t"), yT_acc[0][:], cso[0], reads=[tk("yTa", 0)], writes=[tk("ZT_s")])
                S.flush()

        if "E" in phases:
            sc2b = ExitStack()
            W2box[0] = sbt(sc2b, "W2b_e", [128, 32, D], BF16)
            mlp_phase("E", x2_s, ZT_s, 3, out_d, True, W2box[0], pre_hook=lambda: load_w2(w2_d[1]))
            sc2b.close()
        sc1.close()
    return nc


def _prep_inputs(inputs, NT=4096):
    f = lambda a: np.ascontiguousarray(np.asarray(a, dtype=np.float32))
    x = f(inputs["x"])
    B = x.shape[0]
    shared = {
        "wqkv": f(inputs["attn_w_qkv"][0]), "woa": f(inputs["attn_w_o"][0]),
        "w1_0": f(inputs["mlp_w1"][0]), "w1_1": f(inputs["mlp_w1"][1]),
        "w2_0": f(inputs["mlp_w2"][0]), "w2_1": f(inputs["mlp_w2"][1]),
        "wx": f(inputs["rec_w_x"][0]), "wy": f(inputs["rec_w_y"][0]), "wor": f(inputs["rec_w_o"][0]),
        "wa": f(inputs["rec_w_a"][0]), "wi": f(inputs["rec_w_i"][0]),
    }
    gvec = np.stack([f(inputs["mix_norm_g"])[0], f(inputs["mlp_norm_g"])[0], f(inputs["mix_norm_g"])[1],
                     f(inputs["mlp_norm_g"])[1], f(inputs["final_norm_g"])], axis=0)
    shared["gvec"] = np.ascontiguousarray(gvec)
    pc = lambda v: np.asarray(v, np.float32).reshape(8, 128).T
    cw = f(inputs["rec_conv_w"][0])
    cvec = np.concatenate([pc(cw[0]), pc(cw[1]), pc(cw[2]), pc(cw[3]), pc(inputs["rec_conv_b"][0]), pc(inputs["rec_b_a"][0]),
                           pc(inputs["rec_b_i"][0]), pc(inputs["rec_lambda"][0]),
                           pc(gvec[0]), pc(gvec[1]), pc(gvec[2]), pc(gvec[3])], axis=1)
    shared["cvec"] = np.ascontiguousarray(cvec.astype(np.float32))
    shared["lamv"] = np.ascontiguousarray(np.concatenate([f(inputs["attn_lq1"][0]), f(inputs["attn_lk1"][0]),
                                                          f(inputs["attn_lq2"][0]), f(inputs["attn_lk2"][0])])[None, :])
    shared["subg"] = f(inputs["attn_subln_g"][0])[None, :]
    inv_freq = (1.0 / (np.float32(10000.0) ** (np.arange(0, 64, 2, dtype=np.float32) / np.float32(64.0)))).astype(np.float32)
    ang = np.arange(NT, dtype=np.float32)[:, None] * inv_freq[None, :]
    cos, sin = np.cos(ang).astype(np.float32), np.sin(ang).astype(np.float32)
    shared["rope"] = np.ascontiguousarray(np.concatenate([cos, cos, -sin, sin], axis=1).astype(np.float32))
    shared["ident"] = np.eye(128, dtype=np.float32)
    in_maps = []
    for b in range(B):
        m = dict(shared)
        m["x"] = np.ascontiguousarray(x[b, :NT])
        in_maps.append(m)
    return in_maps


_NC_CACHE = {}


def kernel(**inputs):
    x = np.asarray(inputs["x"])
    B, NT, _ = x.shape
    if NT not in _NC_CACHE:
        _NC_CACHE[NT] = build_nc(NT)
    nc = _NC_CACHE[NT]
    in_maps = _prep_inputs(inputs, NT)
    res = run_bass_kernel_spmd(nc, in_maps, core_ids=list(range(B)))
    out = np.stack([np.asarray(r["out"]).reshape(NT, D) for r in res.results], axis=0)
    return out.astype(np.float32)
```

```python
import numpy as np
from contextlib import ExitStack
import concourse.bass as bass
import concourse.mybir as mybir
from concourse.bass_utils import run_bass_kernel_spmd

F32 = mybir.dt.float32
BF16 = mybir.dt.bfloat16
AF = mybir.ActivationFunctionType
ALU = mybir.AluOpType
AX = mybir.AxisListType


class Tok:
    __slots__ = ("name", "last_w", "readers")

    def __init__(self, name):
        self.name = name
        self.last_w = None
        self.readers = []


class Chan:
    def __init__(self, sem, name):
        self.sem = sem
        self.name = name
        self.count = 0


class Op:
    __slots__ = ("eng", "fn", "reads", "writes", "chan", "deps", "sig", "sigval", "idx")


class Sched:
    ENGS = ("sp", "act", "dve", "pool", "pe")

    def __init__(self, nc, stack):
        self.nc = nc
        self.stack = stack
        self.e = {"sp": nc.sync, "act": nc.scalar, "dve": nc.vector, "pool": nc.gpsimd, "pe": nc.tensor}
        self.prog = {k: stack.enter_context(nc.semaphore("prog_" + k)) for k in self.ENGS if k != "sp"}
        self.sigcount = {k: 0 for k in self.ENGS}
        self.waited = {k: {} for k in self.ENGS}
        self.ops = []
        self.chans = []
        self.nsem = 4

    def chan(self, name):
        c = Chan(self.stack.enter_context(self.nc.semaphore("ch_" + name)), name)
        self.chans.append(c)
        self.nsem += 1
        return c

    def op(self, eng, fn, reads=(), writes=(), chan=None):
        o = Op()
        o.eng, o.fn, o.reads, o.writes, o.chan = eng, fn, tuple(reads), tuple(writes), chan
        o.deps, o.sig, o.sigval = [], False, None
        self.ops.append(o)
        return o

    def dma(self, eng, out, in_, chan, reads=(), writes=()):
        return self.op(eng, lambda e: e.dma_start(out=out, in_=in_), reads, writes, chan)

    def flush(self, barrier=True):
        ops = self.ops
        for k, o in enumerate(ops):
            o.idx = k
            deps = set()
            for r in o.reads:
                if r.last_w is not None:
                    deps.add(r.last_w)
            for w in o.writes:
                if w.last_w is not None:
                    deps.add(w.last_w)
                for rd in w.readers:
                    deps.add(rd)
            deps.discard(k)
            for d in sorted(deps):
                od = ops[d]
                if od.eng == o.eng == "pe" and od.chan is None and o.chan is None:
                    continue
                o.deps.append(od)
                if od.chan is None:
                    od.sig = True
            for r in o.reads:
                r.readers.append(k)
            for w in o.writes:
                w.last_w = k
                w.readers = []
        for o in ops:
            eng = self.e[o.eng]
            wt = self.waited[o.eng]
            for od in o.deps:
                if od.chan is not None:
                    sem, val = od.chan.sem, od.chan.count
                else:
                    sem, val = self.prog[od.eng], od.sigval
                assert val is not None
                key = id(sem)
                if wt.get(key, 0) >= val:
                    continue
                wt[key] = val
                eng.wait_ge(sem, val)
            inst = o.fn(eng)
            if o.chan is not None:
                o.chan.count += 16
                o.sigval = o.chan.count
                inst.then_inc(o.chan.sem, 16)
                o.chan.last_eng = o.eng
            elif o.sig:
                self.sigcount[o.eng] += 1
                o.sigval = self.sigcount[o.eng]
                inst.then_inc(self.prog[o.eng], 1)
        for c in self.chans:
            if c.count and getattr(c, "last_eng", None) is not None:
                wt = self.waited[c.last_eng]
                if wt.get(id(c.sem), 0) < c.count:
                    wt[id(c.sem)] = c.count
                    self.e[c.last_eng].wait_ge(c.sem, c.count)
        if barrier:
            self.nc.all_engine_barrier()
        seen = set()
        for o in ops:
            for t in o.reads + o.writes:
                if id(t) not in seen:
                    seen.add(id(t))
                    t.last_w = None
                    t.readers = []
        self.ops = []


class TK:
    def __init__(self):
        self.d = {}

    def __call__(self, *key):
        t = self.d.get(key)
        if t is None:
            t = self.d[key] = Tok(str(key))
        return t


D = 1024
DFF = 4096
NH = 8
EPS = 1e-6
SUB_EPS = 1e-5
LAMBDA_INIT = 0.8 - 0.6 * 1.0
GK = 0.7978845608028654 * 2.0
GC = 0.044715


def build_nc(NT=4096, debug=False, phases="ABCDE"):
    NTL = NT // 128
    nc = bass.Bass("TRN2", target_bir_lowering=False)
    dt_in = lambda n, s: nc.dram_tensor(n, list(s), F32, kind="ExternalInput").ap()
    skind = "ExternalOutput" if debug else "Internal"
    dt_s = lambda n, s, d: nc.dram_tensor(n, list(s), d, kind=skind).ap()
    x_d = dt_in("x", [NT, D])
    wqkv_d = dt_in("wqkv", [D, 3 * D]); woa_d = dt_in("woa", [D, D])
    w1_d = [dt_in("w1_0", [D, DFF]), dt_in("w1_1", [D, DFF])]
    w2_d = [dt_in("w2_0", [DFF, D]), dt_in("w2_1", [DFF, D])]
    wx_d = dt_in("wx", [D, D]); wy_d = dt_in("wy", [D, D]); wor_d = dt_in("wor", [D, D])
    wa_d = dt_in("wa", [4, 256, 256]); wi_d = dt_in("wi", [4, 256, 256])
    gvec_d = dt_in("gvec", [5, D])
    cvec_d = dt_in("cvec", [128, 96])
    lamv_d = dt_in("lamv", [1, 256]); subg_d = dt_in("subg", [1, 128])
    rope_d = dt_in("rope", [NT, 128]); ident_d = dt_in("ident", [128, 128])
    out_d = nc.dram_tensor("out", [NT, D], F32, kind="ExternalOutput").ap()
    QT_s = dt_s("QT_s", [NH, 128, NT], BF16); KT_s = dt_s("KT_s", [NH, 128, NT], BF16)
    V_s = dt_s("V_s", [NH, 128, NTL, 129], BF16)
    OT_s = dt_s("OT_s", [NH, 128, NT], BF16)
    x2_s = dt_s("x2_s", [NT, D], F32)
    ZT_s = dt_s("ZT_s", [8, 128, NT], BF16)

    with ExitStack() as top:
        S = Sched(nc, top)
        tk = TK()
        ucnt = [0]

        def sbt(st, n, s, d):
            ucnt[0] += 1
            return st.enter_context(nc.sbuf_tensor(f"s{ucnt[0]}_{n}", list(s), d))

        def pst(st, n, s, d):
            ucnt[0] += 1
            return st.enter_context(nc.psum_tensor(f"p{ucnt[0]}_{n}", list(s), d))
        ident = sbt(top, "ident", [128, 128], BF16)
        eps6 = sbt(top, "eps6", [128, 1], F32); eps5 = sbt(top, "eps5", [128, 1], F32)
        cvec = sbt(top, "cvec", [128, 96], F32)
        c_const = S.chan("const")
        c_const2 = S.chan("const2")
        S.dma("pool", ident[:], ident_d, c_const2, writes=[tk("ident")])
        S.dma("sp", cvec[:], cvec_d, c_const, writes=[tk("cvec")])
        S.op("dve", lambda e: e.memset(eps6[:], EPS), writes=[tk("eps6")])
        S.op("dve", lambda e: e.memset(eps5[:], SUB_EPS), writes=[tk("eps5")])
        S.flush()
        CONST_R = [tk("ident"), tk("eps6"), tk("eps5"), tk("cvec")]

        def rstd_ops(ss_ap, out_ap, n, eps_t, toks_r, toks_w):
            S.op("act", lambda e: e.activation(out=out_ap, in_=ss_ap, func=AF.Ln, scale=1.0 / n, bias=eps_t[:]), reads=toks_r, writes=toks_w)
            S.op("act", lambda e: e.activation(out=out_ap, in_=out_ap, func=AF.Exp, scale=-0.5), reads=toks_w, writes=toks_w)

        def gcol_bc(g_row, w=128):
            return cvec[:, 64 + 8 * g_row:72 + 8 * g_row].unsqueeze(2).to_broadcast([128, 8, w])

        def prologue_act(st_bufs, key, xt_ap):
            B = st_bufs
            s = B["pi"] % 2
            B["pi"] += 1
            ss, rstd, xb = B["ss"][s], B["rstd"][s], B["xb"][s]
            S.op("act", lambda e: e.activation(out=xb[:], in_=xt_ap, func=AF.Square, accum_out=ss[:]),
                 reads=[key], writes=[tk("xb", s), tk("ss", s)])
            rstd_ops(ss[:], rstd[:], float(D), eps6, [tk("ss", s)], [tk("rstd", s)])
            S.op("act", lambda e: e.activation(out=xb[:], in_=xt_ap, func=AF.Copy, scale=rstd[:]),
                 reads=[key, tk("rstd", s)], writes=[tk("xb", s)])
            return s

        def prologue_pe(st_bufs, s):
            B = st_bufs
            xb, tp = B["xb"][s], B["tp"][s % len(B["tp"])]
            tps = s % len(B["tp"])

            def tr(e):
                for c in range(8):
                    i = e.transpose(out=tp[:, c, :], in_=xb[:, c * 128:(c + 1) * 128], identity=ident[:])
                return i
            S.op("pe", tr, reads=[tk("xb", s)], writes=[tk("tp", tps)])
            return tp, tk("tp", tps)

        def prologue(st_bufs, key, xt_ap, xT_dst, gbc):
            s = prologue_act(st_bufs, key, xt_ap)
            return prologue_pe(st_bufs, s)

        def mk_prologue_bufs(st, ntp=1):
            B = {"pi": 0}
            B["ss"] = [sbt(st, f"ss{i}", [128, 1], F32) for i in range(2)]
            B["rstd"] = [sbt(st, f"rstd{i}", [128, 1], F32) for i in range(2)]
            B["xb"] = [sbt(st, f"xb{i}", [128, D], BF16) for i in range(2)]
            B["tp"] = [pst(st, f"tp{i}", [128, 8, 128], BF16) for i in range(ntp)]
            return B

        if "A" in phases:
            with ExitStack() as st:
                wq = sbt(st, "wqkv_b", [128, 8, 3 * D], BF16)
                gbc = None
                ropeT = sbt(st, "ropeT", [128, NTL, 128], F32)
                PB = mk_prologue_bufs(st, 1)
                xt = [sbt(st, f"xt{i}", [128, D], F32) for i in range(2)]
                xT = [sbt(st, f"xT{i}", [128, 8, 128], BF16) for i in range(2)]
                ksb = sbt(st, "ksb", [128, D], F32)
                rA = [sbt(st, f"rA{i}", [128, D], F32) for i in range(2)]
                rB = [sbt(st, f"rB{i}", [128, D], F32) for i in range(2)]
                qkb = [[sbt(st, f"qkb{w}{i}", [128, D], BF16) for i in range(2)] for w in range(2)]
                qT_acc = [sbt(st, f"qTa{i}", [128, 8, 512], BF16) for i in range(2)]
                kT_acc = [sbt(st, f"kTa{i}", [128, 8, 512], BF16) for i in range(2)]
                V_acc = [sbt(st, f"Va{i}", [128, 8, 4, 129], BF16) for i in range(2)]
                pq = pst(st, "pq", [128, D], F32); pk = pst(st, "pk", [128, D], F32); pv = pst(st, "pv", [128, D], F32)
                tq = pst(st, "tq", [128, 8, 128], BF16)
                cwq = [S.chan(f"A_w{i}") for i in range(3)]; cg = S.chan("A_g"); cx = [S.chan("A_x0"), S.chan("A_x1")]
                cst = [S.chan("A_st0"), S.chan("A_st1")]
                for cb6 in range(6):
                    S.dma("pool", wq[:, :, cb6 * 512:(cb6 + 1) * 512], wqkv_d[:, cb6 * 512:(cb6 + 1) * 512].rearrange("(c p) f -> p c f", p=128), cwq[cb6 // 2], writes=[tk("wq", cb6)])
                S.dma("sp", ropeT[:], rope_d.rearrange("(i p) f -> p i f", p=128), cg, writes=[tk("ropeT")])
                for i in range(2):
                    S.op("dve", lambda e, i=i: e.memset(V_acc[i][:, :, :, 128:129], 1.0), writes=[tk("Vacc", i)])

                def rope(eng, src_ap, src_tok, dst, dst_tok, i, slot):
                    sv = src_ap.rearrange("p (g d) -> p g d", d=64)
                    cosb = ropeT[:, i, 0:64].unsqueeze(1).to_broadcast([128, 16, 64])
                    sinlo = ropeT[:, i, 64:96].unsqueeze(1).to_broadcast([128, 16, 32])
                    sinhi = ropeT[:, i, 96:128].unsqueeze(1).to_broadcast([128, 16, 32])
                    A = rA[slot][:].rearrange("p (g d) -> p g d", d=64)
                    Bv = rB[slot][:].rearrange("p (g d) -> p g d", d=64)
                    S.op(eng, lambda e: e.tensor_tensor(out=A, in0=sv, in1=cosb, op=ALU.mult),
                         reads=[src_tok, tk("ropeT")], writes=[tk("rA", slot)])
                    S.op(eng, lambda e: e.tensor_tensor(out=Bv[:, :, 0:32], in0=sv[:, :, 32:64], in1=sinlo, op=ALU.mult),
                         reads=[src_tok, tk("ropeT")], writes=[tk("rB", slot, 0)])
                    S.op(eng, lambda e: e.tensor_tensor(out=Bv[:, :, 32:64], in0=sv[:, :, 0:32], in1=sinhi, op=ALU.mult),
                         reads=[src_tok, tk("ropeT")], writes=[tk("rB", slot, 1)])
                    S.op(eng, lambda e: e.tensor_tensor(out=dst[:], in0=rA[slot][:], in1=rB[slot][:], op=ALU.add),
                         reads=[tk("rA", slot), tk("rB", slot, 0), tk("rB", slot, 1)], writes=[dst_tok])

                def proA1(i):
                    s = i % 2
                    S.dma("sp", xt[s][:], x_d[i * 128:(i + 1) * 128, :], cx[s], writes=[tk("xt", s)])
                    return prologue_act(PB, tk("xt", s), xt[s][:])

                def proA2(i, ps):
                    s = i % 2
                    tp, tptok = prologue_pe(PB, ps)
                    S.op("dve", lambda e, s=s, tp=tp: e.tensor_tensor(out=xT[s][:], in0=tp[:], in1=gcol_bc(0), op=ALU.mult), reads=[tptok], writes=[tk("xT", s)])

                def mmA(i):
                    s = i % 2
                    for (pp, name, off) in ((pq, "pq", 0), (pk, "pk", D), (pv, "pv", 2 * D)):
                        def mm(e, pp=pp, off=off, s=s):
                            for n in range(2):
                                for c in range(8):
                                    r = e.matmul(pp[:, n * 512:(n + 1) * 512], lhsT=xT[s][:, c, :], rhs=wq[:, c, off + n * 512: off + (n + 1) * 512],
                                                 start=(c == 0), stop=(c == 7))
                            return r
                        S.op("pe", mm, reads=[tk("xT", s), tk("wq", off // 512), tk("wq", off // 512 + 1)], writes=[tk(name)])

                def evacA1(i):
                    s = i % 2; gi = i // 4; gs = gi % 2; sub = i % 4
                    rope("dve", pq[:], tk("pq"), qkb[0][i % 2], tk("qkb", 0, i % 2), i, 0)
                    S.op("act", lambda e: e.activation(out=ksb[:], in_=pk[:], func=AF.Copy), reads=[tk("pk")], writes=[tk("ksb")])
                    rope("pool", ksb[:], tk("ksb"), qkb[1][i % 2], tk("qkb", 1, i % 2), i, 1)
                    S.op("act", lambda e, gs=gs, sub=sub: e.activation(out=V_acc[gs][:, :, sub, 0:128], in_=pv[:].rearrange("p (h e) -> p h e", e=128), func=AF.Copy),
                         reads=[tk("pv")], writes=[tk("Vacc", gs)])

                def evacA2(i):
                    s = i % 2; gi = i // 4; gs = gi % 2; sub = i % 4
                    for (which, acc, accn) in ((0, qT_acc, "qTa"), (1, kT_acc, "kTa")):
                        def trq(e, which=which, i=i):
                            for h in range(8):
                                r = e.transpose(out=tq[:, h, :], in_=qkb[which][i % 2][:, h * 128:(h + 1) * 128], identity=ident[:])
                            return r
                        S.op("pe", trq, reads=[tk("qkb", which, i % 2)], writes=[tk("tq")])
                        ceng = "dve" if which == 0 else "act"
                        if ceng == "dve":
                            S.op("dve", lambda e, acc=acc, gs=gs, sub=sub: e.tensor_copy(out=acc[gs][:, :, sub * 128:(sub + 1) * 128], in_=tq[:]),
                                 reads=[tk("tq")], writes=[tk(accn, gs)])
                        else:
                            S.op("act", lambda e, acc=acc, gs=gs, sub=sub: e.activation(out=acc[gs][:, :, sub * 128:(sub + 1) * 128], in_=tq[:], func=AF.Copy),
                                 reads=[tk("tq")], writes=[tk(accn, gs)])
                    if sub == 3 or i == NTL - 1:
                        nt = (sub + 1) * 128
                        t0 = gi * 512
                        S.dma("pool", QT_s[:, :, t0:t0 + nt].rearrange("h p t -> p h t"), qT_acc[gs][:, :, 0:nt], cst[gs],
                              reads=[tk("qTa", gs)], writes=[tk("QT_s", gi)])
                        S.dma("pool", KT_s[:, :, t0:t0 + nt].rearrange("h p t -> p h t"), kT_acc[gs][:, :, 0:nt], cst[gs],
                              reads=[tk("kTa", gs)], writes=[tk("KT_s", gi)])
                        S.dma("pool", V_s[:, :, gi * 4:gi * 4 + sub + 1, :].rearrange("h p j e -> p h j e"), V_acc[gs][:, :, 0:sub + 1, :], cst[gs],
                              reads=[tk("Vacc", gs)], writes=[tk("V_s", gi)])


                proA2(0, proA1(0)); mmA(0)
                psn = proA1(1) if NTL > 1 else None
                for i in range(NTL):
                    evacA1(i)
                    if i + 1 < NTL:
                        proA2(i + 1, psn)
                        mmA(i + 1)
                    if i + 2 < NTL:
                        psn = proA1(i + 2)
                    evacA2(i)
                S.flush()

        sc1 = ExitStack(); sc2 = ExitStack()
        W1b = sbt(sc1, "W1b", [128, 8, DFF], BF16); Wzb = sbt(sc1, "Wzb", [128, 8, D], BF16)
        W2box = [sbt(sc2, "W2b", [128, 32, D], BF16)]
        cwm = [S.chan("M_w1"), S.chan("M_w2"), S.chan("M_wz")]

        def load_w1z(w1d, wz_d):
            for c in range(8):
                S.dma("pool", Wzb[:, c, :], wz_d[c * 128:(c + 1) * 128, :], cwm[2], writes=[tk("Wz", c)])
            for c in range(8):
                S.dma("pool", W1b[:, c, :], w1d[c * 128:(c + 1) * 128, :], cwm[0], writes=[tk("W1", c)])

        def load_w2(w2d):
            for c4 in range(8):
                S.dma("pool", W2box[0][:, c4 * 4:(c4 + 1) * 4, :], w2d[c4 * 512:(c4 + 1) * 512, :].rearrange("(c p) d -> p c d", p=128), cwm[1], writes=[tk("W2", c4)])

        if "B" in phases:
            with ExitStack() as st:
                QB = 512
                NQ = NT // QB
                NG = (NT + 511) // 512
                K0p = sbt(st, "K0p", [128, NT], BF16)
                K1p = sbt(st, "K1p", [128, NT], BF16)
                QTq = [sbt(st, f"QTq{i}", [128, QB], BF16) for i in range(2)]
                V_h = sbt(st, "Vh", [128, NTL, 129], BF16)
                NPT = 4
                PT = [sbt(st, f"PT{i}", [128, 2, QB], BF16) for i in range(NPT)]
                lamv = sbt(st, "lamv", [128, 256], F32); g8col = sbt(st, "g8col", [128, 1], F32)
                ltmp = sbt(st, "ltmp", [128, 128], F32); lsum = sbt(st, "lsum", [128, 2], F32)
                nlam = sbt(st, "nlam", [128, 1], F32)
                ones_b = sbt(st, "ones_b", [128, 128], BF16)
                ones_f = sbt(st, "ones_f", [128, 128], F32)
                Oc = [sbt(st, f"Oc{i}", [128, 2, QB], F32) for i in range(2)]
                bcs = [sbt(st, f"bcs{i}", [128, 2, QB], F32) for i in range(2)]
                sq = [sbt(st, f"sq{i}", [128, QB], BF16) for i in range(2)]
                rr = [sbt(st, f"rr{i}", [128, QB], F32) for i in range(2)]
                OTo = [sbt(st, f"OTo{i}", [128, QB], BF16) for i in range(2)]
                Sps = pst(st, "Sps", [128, 2, 2, 512], F32)
                OTp = pst(st, "OTp", [128, 2, 512], F32)
                denp = pst(st, "denp", [128, 2, 512], F32)
                ck = [S.chan(f"B_k{g}") for g in range(NG)]; cv = [S.chan(f"B_v{g}") for g in range(NG)]; cq = [S.chan("B_q0"), S.chan("B_q1")]
                cl = S.chan("B_l"); cs2 = S.chan("B_s"); co = [S.chan("B_o0"), S.chan("B_o1")]; cb = [S.chan("B_b0"), S.chan("B_b1")]
                if "C" in phases:
                    load_w1z(w1_d[0], woa_d); load_w2(w2_d[0])
                S.dma("sp", lamv[:], lamv_d.partition_broadcast(128), cl, writes=[tk("lamv")])
                with nc.allow_non_contiguous_dma(reason="tiny 128-element column load"):
                    S.dma("sp", g8col[:], subg_d.rearrange("o e -> e o"), cs2, writes=[tk("g8col")])
                for i in range(2):
                    S.op("dve", lambda e, i=i: e.tensor_tensor(out=ltmp[:, i * 64:(i + 1) * 64], in0=lamv[:, i * 128:i * 128 + 64],
                                                                in1=lamv[:, i * 128 + 64:i * 128 + 128], op=ALU.mult),
                         reads=[tk("lamv")], writes=[tk("ltmp")])
                    S.op("dve", lambda e, i=i: e.reduce_sum(out=lsum[:, i:i + 1], in_=ltmp[:, i * 64:(i + 1) * 64], axis=AX.X),
                         reads=[tk("ltmp")], writes=[tk("lsum")])
                S.op("act", lambda e: e.activation(out=lsum[:], in_=lsum[:], func=AF.Exp), reads=[tk("lsum")], writes=[tk("lsum")])
                S.op("dve", lambda e: e.tensor_tensor(out=nlam[:], in0=lsum[:, 1:2], in1=lsum[:, 0:1], op=ALU.subtract),
                     reads=[tk("lsum")], writes=[tk("nlam")])
                S.op("dve", lambda e: e.tensor_scalar(out=nlam[:], in0=nlam[:], scalar1=-LAMBDA_INIT, scalar2=None, op0=ALU.add),
                     reads=[tk("nlam")], writes=[tk("nlam")])
                S.op("dve", lambda e: e.tensor_scalar(out=g8col[:], in0=g8col[:], scalar1=1.0 - LAMBDA_INIT, scalar2=None, op0=ALU.mult),
                     reads=[tk("g8col")], writes=[tk("g8col")])
                S.op("dve", lambda e: e.memset(ones_b[:], 1.0), writes=[tk("ones_b")])
                S.op("dve", lambda e: e.memset(ones_f[:], 1.0), writes=[tk("ones_f")])
                S.op("pool", lambda e: e.memset(K0p[64:128, :], 0.0), writes=[tk("Kz")])
                S.op("pool", lambda e: e.memset(K1p[0:64, :], 0.0), writes=[tk("Kz")])

                def load_kv(hh, g):
                    t0 = g * 512; t1 = min(NT, t0 + 512)
                    S.dma("sp", K0p[0:64, t0:t1], KT_s[hh, 0:64, t0:t1], ck[g], writes=[tk("K", g)])
                    S.dma("sp", K1p[64:128, t0:t1], KT_s[hh, 64:128, t0:t1], ck[g], writes=[tk("K", g)])
                    S.dma("sp", V_h[:, t0 // 128:t1 // 128, :], V_s[hh, :, t0 // 128:t1 // 128, :], cv[g], writes=[tk("V", g)])
                SI = [0]
                pidx = 0
                bidx = 0
                pending = []
                GJ = [0]
                t2b = {}

                def release(boundary):
                    held = set()
                    for ent in list(pending):
                        bid_ = ent["bid"]
                        if bid_ in held:
                            continue
                        ok = ent["rel"] <= GJ[0]
                        if ent["k"] == 3 and boundary and bid_ in t2b and GJ[0] >= t2b[bid_] + 2:
                            ok = True
                        if not ok:
                            held.add(bid_)
                            continue
                        pending.remove(ent)
                        ent["fn"]()
                        if ent["k"] == 2:
                            t2b[bid_] = GJ[0]
                        if ent["k"] == 3:
                            for e2 in pending:
                                if e2["bid"] == bid_ and e2["k"] == 4:
                                    e2["rel"] = GJ[0] + 2

                def make_epilogue(h, Q, eb):
                    t0 = Q * QB
                    def s0():
                        for c in range(2):
                            S.op("dve", lambda e, c=c: e.tensor_copy(out=Oc[eb][:, c, :], in_=OTp[:, c, :]), reads=[tk("OTp", c)], writes=[tk("Oc", eb)])
                        for c in range(2):
                            S.op("dve", lambda e, c=c: e.tensor_copy(out=bcs[eb][:, c, :], in_=denp[:, c, :]), reads=[tk("denp", c)], writes=[tk("bcs", eb)])
                    def s1():
                        for c in range(2):
                            S.op("dve", lambda e, c=c: e.reciprocal(out=bcs[eb][:, c, :], in_=bcs[eb][:, c, :]), reads=[tk("bcs", eb)], writes=[tk("bcs", eb)])
                        S.op("dve", lambda e: e.tensor_scalar(out=bcs[eb][:, 1, :], in0=bcs[eb][:, 1, :], scalar1=nlam[:], scalar2=None, op0=ALU.mult),
                             reads=[tk("bcs", eb), tk("nlam")], writes=[tk("bcs", eb)])
                    def s2():
                        for c in range(2):
                            S.op("dve", lambda e, c=c: e.tensor_tensor(out=Oc[eb][:, c, :], in0=Oc[eb][:, c, :], in1=bcs[eb][:, c, :], op=ALU.mult),
                                 reads=[tk("Oc", eb), tk("bcs", eb)], writes=[tk("Oc", eb)])
                        S.op("dve", lambda e: e.tensor_tensor(out=Oc[eb][:, 0, :], in0=Oc[eb][:, 0, :], in1=Oc[eb][:, 1, :], op=ALU.add),
                             reads=[tk("Oc", eb)], writes=[tk("Oc", eb)])
                    def s2b():
                        S.op("act", lambda e: e.activation(out=sq[eb][:], in_=Oc[eb][:, 0, :], func=AF.Square), reads=[tk("Oc", eb)], writes=[tk("sq", eb)])
                    def s3():
                        sl = SI[0] % 2
                        S.op("pe", lambda e: e.matmul(Sps[:, 0, sl, :], lhsT=ones_b[:], rhs=sq[eb][:], start=True, stop=True), reads=[tk("sq", eb), tk("ones_b")], writes=[tk("Sps", sl)])
                        S.op("act", lambda e: e.activation(out=rr[eb][:], in_=Sps[:, 0, sl, :], func=AF.Ln, scale=1.0 / 128.0, bias=eps5[:]), reads=[tk("Sps", sl)], writes=[tk("rr", eb)])
                        S.op("act", lambda e: e.activation(out=rr[eb][:], in_=rr[eb][:], func=AF.Exp, scale=-0.5), reads=[tk("rr", eb)], writes=[tk("rr", eb)])
                    def s4():
                        S.op("dve", lambda e: e.scalar_tensor_tensor(out=OTo[eb][:], in0=Oc[eb][:, 0, :], scalar=g8col[:], in1=rr[eb][:], op0=ALU.mult, op1=ALU.mult),
                             reads=[tk("Oc", eb), tk("rr", eb), tk("g8col")], writes=[tk("OTo", eb)])
                        S.dma("pool", OT_s[h, :, t0:t0 + QB], OTo[eb][:], co[eb], reads=[tk("OTo", eb)], writes=[tk("OT_s", h)])
                    return [s0, s1, s2, s2b, s3, s4]

                def load_q(bi):
                    hh, QQ = bi // NQ, bi % NQ
                    S.dma("sp", QTq[bi % 2][:], QT_s[hh, :, QQ * QB:(QQ + 1) * QB], cq[bi % 2], writes=[tk("QTq", bi % 2)])
                load_q(0)
                for g in range(NG):
                    load_kv(0, g)
                for h in range(NH):
                    hs = h % 2
                    for Q in range(NQ):
                        nj = 4 * Q + 4
                        q0 = Q * QB
                        eb = bidx % 2
                        qs = bidx % 2
                        if bidx + 1 < NH * NQ:
                            load_q(bidx + 1)
                        bidx += 1

                        def emit_S(j, sl):
                            cs = 128 * max(0, j - 4 * Q)

                            def f(e, j=j, sl=sl, qs=qs, cs=cs):
                                e.matmul(Sps[:, 0, sl, cs:QB], lhsT=K0p[:, j * 128:(j + 1) * 128], rhs=QTq[qs][:, cs:QB], start=True, stop=True)
                                return e.matmul(Sps[:, 1, sl, cs:QB], lhsT=K1p[:, j * 128:(j + 1) * 128], rhs=QTq[qs][:, cs:QB], start=True, stop=True)
                            S.op("pe", f, reads=[tk("K", j // 4), tk("Kz"), tk("QTq", qs)], writes=[tk("Sps", sl)])
                        slots = {}
                        slots[0] = SI[0] % 2; SI[0] += 1
                        emit_S(0, slots[0])
                        for j in range(nj):
                            if j + 1 < nj:
                                slots[j + 1] = SI[0] % 2; SI[0] += 1
                                emit_S(j + 1, slots[j + 1])
                            sl = slots[j]
                            ps_ = pidx % NPT; pidx += 1
                            jj = j - 4 * Q
                            c0 = 128 * jj if jj > 0 else 0
                            S.op("act", lambda e, sl=sl, ps_=ps_, c0=c0: e.activation(out=PT[ps_][:, :, c0:QB], in_=Sps[:, :, sl, c0:QB], func=AF.Exp, scale=0.125),
                                 reads=[tk("Sps", sl)], writes=[tk("PT", ps_)])
                            if jj >= 0:
                                S.op("pool", lambda e, ps_=ps_, c0=c0: e.memset(PT[ps_][64:128, :, c0:c0 + 64], 0.0), reads=[], writes=[tk("PT", ps_)])

                            for c in range(2):
                                S.op("pe", lambda e, j=j, ps_=ps_, c0=c0, nj=nj, c=c: e.matmul(OTp[:, c, c0:QB], lhsT=V_h[:, j, 0:128], rhs=PT[ps_][:, c, c0:QB],
                                                                                                  start=(j == 0), stop=(j == nj - 1), skip_group_check=True),
                                     reads=[tk("PT", ps_), tk("V", j // 4)], writes=[tk("OTp", c)])
                            for c in range(2):
                                S.op("pe", lambda e, j=j, ps_=ps_, c0=c0, nj=nj, c=c: e.matmul(denp[:, c, c0:QB], lhsT=ones_b[:], rhs=PT[ps_][:, c, c0:QB],
                                                                                                  start=(j == 0), stop=(j == nj - 1), skip_group_check=True),
                                     reads=[tk("PT", ps_), tk("ones_b")], writes=[tk("denp", c)])
                            if Q == NQ - 1 and j % 4 == 3 and h + 1 < NH:
                                load_kv(h + 1, j // 4)
                            GJ[0] += 1
                            release(False)
                        release(True)
                        for ent in [x for x in pending if x["eb"] == eb]:
                            pending.remove(ent); ent["fn"]()
                        ep = make_epilogue(h, Q, eb)
                        ep[0]()
                        bid = bidx
                        for k, (off, fn) in enumerate(zip((1, 6, 9, 40, 42), ep[1:])):
                            pending.append({"rel": GJ[0] + off, "eb": eb, "fn": fn, "k": k, "bid": bid})
                while pending:
                    pending.pop(0)["fn"]()
                S.flush()

        def mlp_phase(tag, xin_d, ZTs, g_row, xout_d, final, W2b, pre_hook=None):
            with ExitStack() as st:
                NB = NT // 256
                gfb = sbt(st, "gfb", [128, D], F32) if final else None
                ss = [sbt(st, f"ss{i}", [128, 1], F32) for i in range(2)]
                rstd = [sbt(st, f"rstd{i}", [128, 1], F32) for i in range(2)]
                xb = [sbt(st, f"xb{i}", [128, D], BF16) for i in range(2)]
                xn1 = sbt(st, "xn", [128, D], F32) if final else None; xn = [xn1, xn1]
                tp = [pst(st, f"tp{i}", [128, 8, 128], BF16) for i in range(2)]
                xt = [sbt(st, f"xt{i}", [128, 2, D], F32) for i in range(2)]
                zT1 = sbt(st, "zT", [128, 8, 256], BF16); zT = [zT1, zT1]
                xT = [sbt(st, f"xT{i}", [128, 8, 256], BF16) for i in range(2)]
                hidT = sbt(st, "hidT", [128, 32, 256], BF16)
                NR = 2
                rt = [sbt(st, f"rt{i}", [128, 256], F32) for i in range(NR)]
                ss2 = sbt(st, "ss2", [128, 1], F32); rs2 = sbt(st, "rs2", [128, 1], F32)
                yps = [pst(st, f"yps{i}", [128, D], F32) for i in range(2)]
                zps = [pst(st, f"zps{i}", [128, 512], F32) for i in range(2)]
                cg = S.chan(tag + "_g"); cgf = S.chan(tag + "_gf")
                cx = [S.chan(tag + "_x0"), S.chan(tag + "_x1")]; cz = [S.chan(tag + "_z0"), S.chan(tag + "_z1")]
                cso = [S.chan(tag + "_o0"), S.chan(tag + "_o1")]
                if pre_hook is not None:
                    pre_hook()
                if final:
                    S.dma("sp", gfb[:], gvec_d[4:5, :].partition_broadcast(128), cgf, writes=[tk("gfb")])
                W1r = [tk("W1", c) for c in range(8)]; Wzr = [tk("Wz", c) for c in range(8)]; W2r = [tk("W2", c) for c in range(8)]
                cnt = {"yi": 0, "zi": 0, "ri": 0}

                def load_x(b):
                    s = b % 2; t0 = b * 256
                    S.dma("sp", xt[s][:], xin_d[t0:t0 + 256, :].rearrange("(t p) d -> p t d", p=128), cx[s], writes=[tk("xt", s, 0), tk("xt", s, 1)])

                def load_z(b):
                    s = b % 2; t0 = b * 256
                    S.dma("sp", zT[s][:], ZTs[:, :, t0:t0 + 256].rearrange("c p t -> p c t"), cz[0], writes=[tk("zT")])

                def pre_y0(b):
                    s = b % 2
                    for t in range(2):
                        ys = cnt["yi"] % 2; cnt["yi"] += 1

                        def mm0(e, s=s, t=t, ys=ys):
                            for n in range(2):
                                for c in range(8):
                                    r = e.matmul(yps[ys][:, n * 512:(n + 1) * 512], lhsT=zT[s][:, c, t * 128:(t + 1) * 128], rhs=Wzb[:, c, n * 512:(n + 1) * 512],
                                                 start=(c == 0), stop=(c == 7))
                            return r
                        S.op("pe", mm0, reads=[tk("zT")] + Wzr, writes=[tk("yps", ys)])
                        S.op("dve", lambda e, s=s, t=t, ys=ys: e.tensor_tensor(out=xt[s][:, t, :], in0=xt[s][:, t, :], in1=yps[ys][:], op=ALU.add),
                             reads=[tk("yps", ys), tk("xt", s, t)], writes=[tk("xt", s, t)])

                def pre_norm(b):
                    s = b % 2
                    for t in range(2):
                        S.op("act", lambda e, s=s, t=t: e.activation(out=xb[t][:], in_=xt[s][:, t, :], func=AF.Square, accum_out=ss[t][:]),
                             reads=[tk("xt", s, t)], writes=[tk("xb", t), tk("ss", t)])
                        rstd_ops(ss[t][:], rstd[t][:], float(D), eps6, [tk("ss", t)], [tk("rstd", t)])
                        S.op("act", lambda e, s=s, t=t: e.activation(out=xb[t][:], in_=xt[s][:, t, :], func=AF.Copy, scale=rstd[t][:]),
                             reads=[tk("xt", s, t), tk("rstd", t)], writes=[tk("xb", t)])

                def pre_mult(b):
                    pass

                def pre_tr(b):
                    for t in range(2):
                        def tr(e, t=t):
                            for c in range(8):
                                i = e.transpose(out=tp[t][:, c, :], in_=xb[t][:, c * 128:(c + 1) * 128], identity=ident[:])
                            return i
                        S.op("pe", tr, reads=[tk("xb", t)], writes=[tk("tp", t)])

                def pre_copy(b):
                    for t in range(2):
                        S.op("dve", lambda e, t=t, b=b: e.tensor_tensor(out=xT[b % 2][:, :, t * 128:(t + 1) * 128], in0=tp[t][:], in1=gcol_bc(g_row), op=ALU.mult),
                             reads=[tk("tp", t)], writes=[tk("xT", b % 2)])

                def mm1_range(b, f0, f1):
                    for f in range(f0, f1):
                        zs = cnt["zi"] % 2; cnt["zi"] += 1
                        zp = zps[zs][:, 0:256]

                        def mm1(e, f=f, zp=zp, b=b):
                            for c in range(8):
                                r = e.matmul(zp, lhsT=W1b[:, c, f * 128:(f + 1) * 128], rhs=xT[b % 2][:, c, :], start=(c == 0), stop=(c == 7))
                            return r
                        S.op("pe", mm1, reads=[tk("xT", b % 2)] + W1r, writes=[tk("zps", zs)])
                        r_ = cnt["ri"] % NR; cnt["ri"] += 1
                        S.op("act", lambda e, zp=zp, r_=r_: e.activation(out=rt[r_][:], in_=zp, func=AF.Relu), reads=[tk("zps", zs)], writes=[tk("rt", r_)])
                        S.op("pool", lambda e, f=f, r_=r_: e.tensor_tensor(out=hidT[:, f, :], in0=rt[r_][:], in1=rt[r_][:], op=ALU.mult),
                             reads=[tk("rt", r_)], writes=[tk("hidT", f)])

                def mm2_t(b, t):
                    s = b % 2
                    ys = cnt["yi"] % 2; cnt["yi"] += 1
                    for n in range(2):
                        for fg in range(4):
                            def mm2(e, t=t, ys=ys, n=n, fg=fg):
                                for f in range(fg * 8, fg * 8 + 8):
                                    r = e.matmul(yps[ys][:, n * 512:(n + 1) * 512], lhsT=hidT[:, f, t * 128:(t + 1) * 128], rhs=W2b[:, f, n * 512:(n + 1) * 512],
                                                 start=(f == 0), stop=(f == 31))
                                return r
                            S.op("pe", mm2, reads=[tk("hidT", f) for f in range(fg * 8, fg * 8 + 8)] + W2r, writes=[tk("yps", ys)])
                    S.op("dve", lambda e, s=s, t=t, ys=ys: e.tensor_tensor(out=xt[s][:, t, :], in0=xt[s][:, t, :], in1=yps[ys][:], op=ALU.add),
                         reads=[tk("yps", ys), tk("xt", s, t)], writes=[tk("xt", s, t)])
                    if final:
                        S.op("act", lambda e, s=s, t=t: e.activation(out=xn[t][:], in_=xt[s][:, t, :], func=AF.Square, accum_out=ss2[:]),
                             reads=[tk("xt", s, t)], writes=[tk("xn"), tk("ss2")])
                        rstd_ops(ss2[:], rs2[:], float(D), eps6, [tk("ss2")], [tk("rs2")])
                        S.op("dve", lambda e, s=s, t=t: e.scalar_tensor_tensor(out=xt[s][:, t, :], in0=xt[s][:, t, :], scalar=rs2[:], in1=gfb[:], op0=ALU.mult, op1=ALU.mult),
                             reads=[tk("xt", s, t), tk("rs2"), tk("gfb")], writes=[tk("xt", s, t)])

                def store(b):
                    s = b % 2; t0 = b * 256
                    S.dma("sp", xout_d[t0:t0 + 256, :].rearrange("(t p) d -> p t d", p=128), xt[s][:], cso[s], reads=[tk("xt", s, 0), tk("xt", s, 1)], writes=[tk("xout", b)])

                load_x(0); load_z(0)
                if NB > 1:
                    load_x(1)
                pre_y0(0)
                if NB > 1:
                    load_z(1)
                pre_norm(0); pre_mult(0); pre_tr(0); pre_copy(0)
                for b in range(NB):
                    nxt = b + 1 < NB
                    mm1_range(b, 0, 8)
                    if nxt:
                        pre_y0(b + 1)
                        if b + 2 < NB:
                            load_z(b + 2)
                    mm1_range(b, 8, 16)
                    if nxt:
                        pre_norm(b + 1)
                    mm1_range(b, 16, 24)
                    if nxt:
                        pre_mult(b + 1)
                    mm1_range(b, 24, 32)
                    if nxt:
                        pre_tr(b + 1)
                    mm2_t(b, 0)
                    if nxt:
                        pre_copy(b + 1)
                    mm2_t(b, 1)
                    store(b)
                    if b + 2 < NB:
                        load_x(b + 2)
                S.flush()

        if "C" in phases:
            hook = None if "B" in phases else (lambda: (load_w1z(w1_d[0], woa_d), load_w2(w2_d[0])))
            mlp_phase("C", x_d, OT_s, 1, x2_s, False, W2box[0], pre_hook=hook)
        sc2.close()

        if "D" in phases:
            with ExitStack() as st:
                TB = 512 if NT >= 512 else NT
                NBK = NT // TB
                NTB = TB // 128
                Wxb = sbt(st, "Wxb", [128, 8, D], BF16); Wyb = sbt(st, "Wyb", [128, 8, D], BF16)
                Wab = sbt(st, "Wab", [128, 4, 2, 256], BF16); Wib = sbt(st, "Wib", [128, 4, 2, 256], BF16)
                gbc = None
                PB = mk_prologue_bufs(st, 1)
                xt = [sbt(st, f"xt{i}", [128, D], F32) for i in range(2)]
                xT = [sbt(st, f"xT{i}", [128, 8, TB], BF16) for i in range(2)]
                xpre = [sbt(st, f"xpre{i}", [128, TB + 3], F32) for i in range(4)]
                halo = sbt(st, "halo", [128, 8, 4], F32); one1 = sbt(st, "one1", [128, 1], F32)
                xc = [sbt(st, f"xc{i}", [128, TB], F32) for i in range(4)]
                xcb = [sbt(st, f"xcb{i}", [128, TB], BF16) for i in range(4)]
                gate = [sbt(st, f"gate{i}", [128, TB], BF16) for i in range(4)]
                g1 = [sbt(st, f"g1{i}", [128, TB], F32) for i in range(2)]
                ga = [sbt(st, f"ga{i}", [128, TB], F32) for i in range(2)]; gi_ = [sbt(st, f"gi{i}", [128, TB], F32) for i in range(2)]
                aa = [sbt(st, f"aa{i}", [128, TB], F32) for i in range(2)]; m2 = [sbt(st, f"m2{i}", [128, TB], F32) for i in range(2)]
                hs_ = [sbt(st, f"hs{i}", [128, TB], F32) for i in range(2)]
                yT_acc = [sbt(st, "yTa0", [128, 8, TB], BF16)]
                hlast = sbt(st, "hlast", [128, 8], F32)
                nb = sbt(st, "nb", [128, 16], F32)
                sp8 = sbt(st, "sp8", [128, 8], F32); spe = sbt(st, "spe", [128, 8], F32); spt = sbt(st, "spt", [128, 8], F32)
                psx1 = pst(st, "psx0", [128, 512], F32); psx = [psx1, psx1]
                psy = [pst(st, f"psy{i}", [128, 512], F32) for i in range(2)]
                psa = [pst(st, f"psa{i}", [128, 512], F32) for i in range(2)]; psi = [pst(st, f"psi{i}", [128, 512], F32) for i in range(2)]
                cw = [S.chan("D_wx"), S.chan("D_wy"), S.chan("D_wa"), S.chan("D_wi")]
                cg = S.chan("D_g"); cx = [S.chan("D_x0"), S.chan("D_x1")]; cso = [S.chan("D_o0"), S.chan("D_o1")]
                for n in range(4):
                    S.dma("pool", Wxb[:, :, n * 256:(n + 1) * 256], wx_d[:, n * 256:(n + 1) * 256].rearrange("(c p) f -> p c f", p=128), cw[n], writes=[tk("Wx", n)])
                    S.dma("pool", Wyb[:, :, n * 256:(n + 1) * 256], wy_d[:, n * 256:(n + 1) * 256].rearrange("(c p) f -> p c f", p=128), cw[n], writes=[tk("Wy", n)])
                    S.dma("pool", Wab[:, n, :, :], wa_d[n].rearrange("(i p) o -> p i o", p=128), cw[n], writes=[tk("Wa", n)])
                    S.dma("pool", Wib[:, n, :, :], wi_d[n].rearrange("(i p) o -> p i o", p=128), cw[n], writes=[tk("Wi", n)])
                if "E" in phases:
                    load_w1z(w1_d[1], wor_d)
                S.op("dve", lambda e: e.tensor_scalar(out=nb[:], in0=cvec[:, 40:56], scalar1=-1.0, scalar2=None, op0=ALU.mult), writes=[tk("nb")])
                S.op("act", lambda e: e.activation(out=spe[:], in_=cvec[:, 56:64], func=AF.Exp, scale=-1.0), writes=[tk("spe")])
                S.op("dve", lambda e: e.tensor_scalar(out=spt[:], in0=spe[:], scalar1=1.0 / 3.0, scalar2=-0.5, op0=ALU.mult, op1=ALU.add), reads=[tk("spe")], writes=[tk("spt")])
                S.op("dve", lambda e: e.tensor_tensor(out=spt[:], in0=spt[:], in1=spe[:], op=ALU.mult), reads=[tk("spt"), tk("spe")], writes=[tk("spt")])
                S.op("dve", lambda e: e.tensor_scalar(out=spt[:], in0=spt[:], scalar1=1.0, scalar2=None, op0=ALU.add), reads=[tk("spt")], writes=[tk("spt")])
                S.op("dve", lambda e: e.tensor_tensor(out=spt[:], in0=spt[:], in1=spe[:], op=ALU.mult), reads=[tk("spt"), tk("spe")], writes=[tk("spt")])
                S.op("dve", lambda e: e.tensor_scalar(out=sp8[:], in0=spt[:], scalar1=-8.0, scalar2=None, op0=ALU.mult), reads=[tk("spt")], writes=[tk("sp8")])
                S.op("dve", lambda e: e.memset(hlast[:], 0.0), writes=[tk("hlast")])
                S.op("pool", lambda e: e.memset(halo[:], 0.0), writes=[tk("halo", c) for c in range(8)])
                S.op("pool", lambda e: e.memset(one1[:], 1.0), writes=[tk("one1")])
                XI = [0]

                def pro(blk):
                    for t in range(NTB):
                        s = XI[0] % 2; XI[0] += 1
                        i = blk * NTB + t
                        S.dma("sp", xt[s][:], x2_s[i * 128:(i + 1) * 128, :], cx[s], reads=[tk("x2")], writes=[tk("xt", s)])
                        tp, tptok = prologue(PB, tk("xt", s), xt[s][:], None, gbc)
                        S.op("dve", lambda e, t=t, tp=tp, blk=blk: e.tensor_tensor(out=xT[blk % 2][:, :, t * 128:(t + 1) * 128], in0=tp[:], in1=gcol_bc(2), op=ALU.mult),
                             reads=[tptok], writes=[tk("xT", blk % 2)])

                def stA(g):
                    blk, n = g // 4, g % 4
                    pair = (2 * n, 2 * n + 1)
                    for cc in pair:
                        xs = cc % 2
                        for (ps_, W_, nm, xk) in ((psx[xs], Wxb, "psx", 0), (psy[xs], Wyb, "psy", xs)):
                            def mmxy(e, ps_=ps_, W_=W_, cc=cc, blk=blk):
                                for c in range(8):
                                    r = e.matmul(ps_[:, 0:TB], lhsT=W_[:, c, cc * 128:(cc + 1) * 128], rhs=xT[blk % 2][:, c, :], start=(c == 0), stop=(c == 7))
                                return r
                            S.op("pe", mmxy, reads=[tk("xT", blk % 2), tk("Wx", cc // 2), tk("Wy", cc // 2)], writes=[tk(nm, xk)])
                        q4 = cc % 4
                        S.op("dve", lambda e, cc=cc, q4=q4: e.tensor_copy(out=xpre[q4][:, 0:3], in_=halo[:, cc, 0:3]), reads=[tk("halo", cc)], writes=[tk("xpre", q4)])
                        S.op("act", lambda e, q4=q4, xs=xs: e.activation(out=xpre[q4][:, 3:TB + 3], in_=psx[xs][:, 0:TB], func=AF.Copy),
                             reads=[tk("psx", 0)], writes=[tk("xpre", q4)])
                        S.op("dve", lambda e, cc=cc, q4=q4: e.tensor_copy(out=halo[:, cc, 0:3], in_=xpre[q4][:, TB:TB + 3]), reads=[tk("xpre", q4)], writes=[tk("halo", cc)])

                def stB(g):
                    blk, n = g // 4, g % 4
                    pair = (2 * n, 2 * n + 1)
                    for cc in pair:
                        xs = cc % 2; q4 = cc % 4
                        S.op("act", lambda e, xs=xs: e.activation(out=g1[xs][:], in_=psy[xs][:, 0:TB], func=AF.Square), reads=[tk("psy", xs)], writes=[tk("g1", xs)])

                def stC(g):
                    blk, n = g // 4, g % 4
                    pair = (2 * n, 2 * n + 1)
                    for cc in pair:
                        q4 = cc % 4
                        S.op("pool", lambda e, cc=cc, q4=q4: e.tensor_scalar(out=xc[q4][:], in0=xpre[q4][:, 3:TB + 3], scalar1=cvec[:, 24 + cc:25 + cc], scalar2=cvec[:, 32 + cc:33 + cc],
                                                                              op0=ALU.mult, op1=ALU.add), reads=[tk("xpre", q4)], writes=[tk("xc", q4)])
                        for j in (2, 1, 0):
                            S.op("dve", lambda e, cc=cc, q4=q4, j=j: e.scalar_tensor_tensor(out=xc[q4][:], in0=xpre[q4][:, j:TB + j], scalar=cvec[:, j * 8 + cc:j * 8 + cc + 1], in1=xc[q4][:],
                                                                                             op0=ALU.mult, op1=ALU.add), reads=[tk("xpre", q4), tk("xc", q4)], writes=[tk("xc", q4)])
                        S.op("act", lambda e, q4=q4: e.activation(out=xcb[q4][:], in_=xc[q4][:], func=AF.Copy), reads=[tk("xc", q4)], writes=[tk("xcb", q4)])

                def stD(g):
                    blk, n = g // 4, g % 4
                    pair = (2 * n, 2 * n + 1)
                    for cc in pair:
                        xs = cc % 2
                        S.op("dve", lambda e, xs=xs: e.tensor_scalar(out=g1[xs][:], in0=g1[xs][:], scalar1=GC, scalar2=1.0, op0=ALU.mult, op1=ALU.add),
                             reads=[tk("g1", xs)], writes=[tk("g1", xs)])
                        S.op("dve", lambda e, xs=xs: e.tensor_tensor(out=g1[xs][:], in0=g1[xs][:], in1=psy[xs][:, 0:TB], op=ALU.mult),
                             reads=[tk("g1", xs), tk("psy", xs)], writes=[tk("g1", xs)])
                        S.op("act", lambda e, xs=xs: e.activation(out=g1[xs][:], in_=g1[xs][:], func=AF.Sigmoid, scale=GK), reads=[tk("g1", xs)], writes=[tk("g1", xs)])
                        S.op("dve", lambda e, xs=xs, cc=cc: e.tensor_tensor(out=gate[cc % 4][:], in0=g1[xs][:], in1=psy[xs][:, 0:TB], op=ALU.mult),
                             reads=[tk("g1", xs), tk("psy", xs)], writes=[tk("gate", cc % 4)])

                def stE(g):
                    blk, n = g // 4, g % 4
                    pair = (2 * n, 2 * n + 1)
                    for oc in pair:
                        ol = oc % 2
                        for (pg, Wg, nm) in ((psa[ol], Wab, "psa"), (psi[ol], Wib, "psi")):
                            def mmg(e, pg=pg, Wg=Wg, n=n, ol=ol):
                                for l in range(2):
                                    r = e.matmul(pg[:, 0:TB], lhsT=Wg[:, n, l, ol * 128:(ol + 1) * 128], rhs=xcb[(2 * n + l) % 4][:], start=(l == 0), stop=(l == 1))
                                return r
                            S.op("pe", mmg, reads=[tk("xcb", (2 * n) % 4), tk("xcb", (2 * n + 1) % 4), tk("Wa", n), tk("Wi", n)], writes=[tk(nm, ol)])
                        S.op("act", lambda e, oc=oc, ol=ol: e.activation(out=ga[ol][:], in_=psa[ol][:, 0:TB], func=AF.Sigmoid, bias=cvec[:, 40 + oc:41 + oc]),
                             reads=[tk("psa", ol)], writes=[tk("ga", ol)])
                        S.op("act", lambda e, oc=oc, ol=ol: e.activation(out=gi_[ol][:], in_=psi[ol][:, 0:TB], func=AF.Sigmoid, bias=cvec[:, 48 + oc:49 + oc]),
                             reads=[tk("psi", ol)], writes=[tk("gi", ol)])
                        S.op("pool", lambda e, ol=ol, oc=oc: e.tensor_tensor(out=gi_[ol][:], in0=gi_[ol][:], in1=xc[oc % 4][:], op=ALU.mult), reads=[tk("gi", ol), tk("xc", oc % 4)], writes=[tk("gi", ol)])

                def stFG(g):
                    blk, n = g // 4, g % 4
                    pair = (2 * n, 2 * n + 1)
                    for oc in pair:
                        ol = oc % 2
                        S.op("act", lambda e, oc=oc, ol=ol: e.activation(out=aa[ol][:], in_=ga[ol][:], func=AF.Exp, scale=sp8[:, oc:oc + 1]), reads=[tk("ga", ol), tk("sp8")], writes=[tk("aa", ol)])
                    for oc in pair:
                        ol = oc % 2
                        S.op("act", lambda e, ol=ol: e.activation(out=m2[ol][:], in_=aa[ol][:], func=AF.Square), reads=[tk("aa", ol)], writes=[tk("m2", ol)])
                    for oc in pair:
                        ol = oc % 2
                        S.op("act", lambda e, ol=ol: e.activation(out=m2[ol][:], in_=m2[ol][:], func=AF.Ln, scale=-1.0, bias=one1[:]), reads=[tk("m2", ol)], writes=[tk("m2", ol)])
                    for oc in pair:
                        ol = oc % 2
                        S.op("act", lambda e, ol=ol: e.activation(out=m2[ol][:], in_=m2[ol][:], func=AF.Exp, scale=0.5), reads=[tk("m2", ol)], writes=[tk("m2", ol)])

                def stH(g):
                    blk, n = g // 4, g % 4
                    pair = (2 * n, 2 * n + 1)
                    for oc in pair:
                        ol = oc % 2; o4 = oc % 4
                        S.op("dve", lambda e, ol=ol: e.tensor_tensor(out=gi_[ol][:], in0=gi_[ol][:], in1=m2[ol][:], op=ALU.mult), reads=[tk("gi", ol), tk("m2", ol)], writes=[tk("gi", ol)])
                        S.op("dve", lambda e, oc=oc, ol=ol: e.tensor_tensor_scan(out=hs_[ol][:], data0=aa[ol][:], data1=gi_[ol][:], initial=hlast[:, oc:oc + 1], op0=ALU.mult, op1=ALU.add),
                             reads=[tk("aa", ol), tk("gi", ol), tk("hlast")], writes=[tk("hs", ol)])
                        S.op("dve", lambda e, oc=oc, ol=ol: e.tensor_copy(out=hlast[:, oc:oc + 1], in_=hs_[ol][:, TB - 1:TB]), reads=[tk("hs", ol)], writes=[tk("hlast")])
                        S.op("pool", lambda e, oc=oc, ol=ol, o4=o4: e.tensor_tensor(out=yT_acc[0][:, oc, :], in0=hs_[ol][:], in1=gate[o4][:], op=ALU.mult),
                             reads=[tk("hs", ol), tk("gate", o4)], writes=[tk("yTa", 0)])

                G = NBK * 4
                pro(0); stA(0); stB(0); stC(0); stD(0)
                for g in range(G):
                    stE(g)
                    if g % 4 == 1 and g // 4 + 1 < NBK:
                        pro(g // 4 + 1)
                    if g + 1 < G:
                        stA(g + 1); stB(g + 1)
                    stFG(g)
                    if g + 1 < G:
                        stC(g + 1)
                        stD(g + 1)
                    stH(g)
                    if g % 4 == 3:
                        blk = g // 4
                        S.dma("sp", ZT_s[:, :, blk * TB:(blk + 1) * TB].rearrange("c p t -> p c t"), yT_acc[0][:], cso[0], reads=[tk("yTa", 0)], writes=[tk("ZT_s")])
                S.flush()

        if "E" in phases:
            sc2b = ExitStack()
            W2box[0] = sbt(sc2b, "W2b_e", [128, 32, D], BF16)
            mlp_phase("E", x2_s, ZT_s, 3, out_d, True, W2box[0], pre_hook=lambda: load_w2(w2_d[1]))
            sc2b.close()
        sc1.close()
    return nc


def _prep_inputs(inputs, NT=4096):
    f = lambda a: np.ascontiguousarray(np.asarray(a, dtype=np.float32))
    x = f(inputs["x"])
    B = x.shape[0]
    shared = {
        "wqkv": f(inputs["attn_w_qkv"][0]), "woa": f(inputs["attn_w_o"][0]),
        "w1_0": f(inputs["mlp_w1"][0]), "w1_1": f(inputs["mlp_w1"][1]),
        "w2_0": f(inputs["mlp_w2"][0]), "w2_1": f(inputs["mlp_w2"][1]),
        "wx": f(inputs["rec_w_x"][0]), "wy": f(inputs["rec_w_y"][0]), "wor": f(inputs["rec_w_o"][0]),
        "wa": f(inputs["rec_w_a"][0]), "wi": f(inputs["rec_w_i"][0]),
    }
    gvec = np.stack([f(inputs["mix_norm_g"])[0], f(inputs["mlp_norm_g"])[0], f(inputs["mix_norm_g"])[1],
                     f(inputs["mlp_norm_g"])[1], f(inputs["final_norm_g"])], axis=0)
    shared["gvec"] = np.ascontiguousarray(gvec)
    pc = lambda v: np.asarray(v, np.float32).reshape(8, 128).T
    cw = f(inputs["rec_conv_w"][0])
    cvec = np.concatenate([pc(cw[0]), pc(cw[1]), pc(cw[2]), pc(cw[3]), pc(inputs["rec_conv_b"][0]), pc(inputs["rec_b_a"][0]),
                           pc(inputs["rec_b_i"][0]), pc(inputs["rec_lambda"][0]),
                           pc(gvec[0]), pc(gvec[1]), pc(gvec[2]), pc(gvec[3])], axis=1)
    shared["cvec"] = np.ascontiguousarray(cvec.astype(np.float32))
    shared["lamv"] = np.ascontiguousarray(np.concatenate([f(inputs["attn_lq1"][0]), f(inputs["attn_lk1"][0]),
                                                          f(inputs["attn_lq2"][0]), f(inputs["attn_lk2"][0])])[None, :])
    shared["subg"] = f(inputs["attn_subln_g"][0])[None, :]
    inv_freq = (1.0 / (np.float32(10000.0) ** (np.arange(0, 64, 2, dtype=np.float32) / np.float32(64.0)))).astype(np.float32)
    ang = np.arange(NT, dtype=np.float32)[:, None] * inv_freq[None, :]
    cos, sin = np.cos(ang).astype(np.float32), np.sin(ang).astype(np.float32)
    shared["rope"] = np.ascontiguousarray(np.concatenate([cos, cos, -sin, sin], axis=1).astype(np.float32))
    shared["ident"] = np.eye(128, dtype=np.float32)
    in_maps = []
    for b in range(B):
        m = dict(shared)
        m["x"] = np.ascontiguousarray(x[b, :NT])
        in_maps.append(m)
    return in_maps


_NC_CACHE = {}


def kernel(**inputs):
    x = np.asarray(inputs["x"])
    B, NT, _ = x.shape
    if NT not in _NC_CACHE:
        _NC_CACHE[NT] = build_nc(NT)
    nc = _NC_CACHE[NT]
    in_maps = _prep_inputs(inputs, NT)
    res = run_bass_kernel_spmd(nc, in_maps, core_ids=list(range(B)))
    out = np.stack([np.asarray(r["out"]).reshape(NT, D) for r in res.results], axis=0)
    return out.astype(np.float32)
```
